# Optimizing a Trainium2 kernel written in Bass

```python
import math
import jax, jax.numpy as jnp
from jax import lax
import numpy as np

D_MODEL = 2048
BATCH = 4
SEQ = 8192
DEPTH = 4

GRID_W = 64
CTX_LEN = 256
RMS_EPS = 1e-6
POS_BASE = 10000.0
ADA_CHUNKS = 6

M_D_INNER = 2 * D_MODEL
M_HEADDIM = 64
M_HEADS = M_D_INNER // M_HEADDIM
M_GROUPS = 8
M_STATE = 128
M_CONV = 5
M_CHUNK = 128
M_GN = M_GROUPS * M_STATE
M_XBC = M_D_INNER + 2 * M_GN
M_DT_MIN = 1e-3
M_DT_MAX = 1e-1

H_WIDTH = D_MODEL
H_SHORT = 3
H_EMB = 33
H_BANDS = (H_EMB - 1) // 2
H_HIDDEN = 64
H_DECAY_TARGET = 1e-2
H_FAST_PCT = 0.3
H_SLOW_PCT = 1.5

FFN_HIDDEN = -(-8 * D_MODEL // (3 * 256)) * 256

OFF_DT = M_XBC
OFF_Z = OFF_DT + 2 * M_HEADS
OFF_HY = OFF_Z + M_D_INNER
OFF_GATE = OFF_HY + 3 * H_WIDTH
IN_COLS = OFF_GATE + 2 * D_MODEL

kernel_name = 'hybrid_ssd_hyena_prefix_trunk'


def rmsnorm(x, g):
    xf = x.astype(jnp.float32)
    y = xf * lax.rsqrt(jnp.mean(xf * xf, axis=-1, keepdims=True) + RMS_EPS)
    return (y * g.astype(jnp.float32)).astype(x.dtype)


def modulated_norm(x, g, shift, scale):
    return rmsnorm(x, g) * (1.0 + scale) + shift


def centred_dwconv(u, w, b):
    k = w.shape[0]
    pad = k // 2
    L = u.shape[1]
    up = jnp.pad(u, ((0, 0), (pad, pad), (0, 0)))
    y = up[:, 0:L] * w[0]
    for j in range(1, k):
        y = y + up[:, j:j + L] * w[j]
    return y + b


def grid_pos_embed(rows, d, dtype):
    r, col = jnp.meshgrid(jnp.arange(rows), jnp.arange(GRID_W), indexing='ij')
    quarter = d // 4
    omega = 1.0 / (POS_BASE ** (jnp.arange(quarter, dtype=jnp.float32) / quarter))

    def axis_embed(pos):
        ang = pos.reshape(-1)[:, None].astype(jnp.float32) * omega[None, :]
        return jnp.concatenate([jnp.sin(ang), jnp.cos(ang)], axis=-1)

    return jnp.concatenate([axis_embed(r), axis_embed(col)], axis=-1).astype(dtype)


def ssd_scan(x, dt, a, B, C, init, with_output):
    dtype = x.dtype
    f32 = jnp.float32
    b, L, H, P = x.shape
    G, N = B.shape[2], B.shape[3]
    R = H // G
    T = M_CHUNK
    nc = L // T
    xd = (x.astype(f32) * dt[..., None]).reshape(b, nc, T, G, R, P)
    Bc = B.astype(f32).reshape(b, nc, T, G, N)
    Cc = C.astype(f32).reshape(b, nc, T, G, N)
    a_cs = jnp.cumsum((dt * a).reshape(b, nc, T, G, R).transpose(0, 3, 4, 1, 2), axis=-1)
    decay_to_end = jnp.exp(a_cs[..., -1:] - a_cs)
    chunk_states = jnp.einsum('bctgn,bgrct,bctgrp->bcgrpn', Bc, decay_to_end, xd)
    chunk_decay = jnp.exp(a_cs[..., -1])

    def step(state, inp):
        dec, st = inp
        return dec[..., None, None] * state + st, state

    final, prev = lax.scan(step, init, (jnp.moveaxis(chunk_decay, -1, 0), jnp.moveaxis(chunk_states, 1, 0)))
    if not with_output:
        return None, final
    prev = jnp.moveaxis(prev, 0, 1)
    seg = a_cs[..., :, None] - a_cs[..., None, :]
    lower = jnp.tril(jnp.ones((T, T), dtype=bool))
    decay_in = jnp.exp(jnp.where(lower, seg, -jnp.inf))
    cb = jnp.einsum('bctgn,bcsgn->bgcts', Cc, Bc)
    y_diag = jnp.einsum('bgrcts,bcsgrp->bctgrp', cb[:, :, None] * decay_in, xd)
    y_off = jnp.einsum('bctgn,bcgrpn,bgrct->bctgrp', Cc, prev, jnp.exp(a_cs))
    return (y_diag + y_off).reshape(b, L, H, P).astype(dtype), final


def mamba_inputs(xbc_raw, dt_raw, conv_w, conv_b, dt_bias, a_log):
    b, L, _ = xbc_raw.shape
    xbc = jax.nn.silu(centred_dwconv(xbc_raw, conv_w, conv_b))
    xs = xbc[..., :M_D_INNER].reshape(b, L, M_HEADS, M_HEADDIM)
    Bm = xbc[..., M_D_INNER:M_D_INNER + M_GN].reshape(b, L, M_GROUPS, M_STATE)
    Cm = xbc[..., M_D_INNER + M_GN:].reshape(b, L, M_GROUPS, M_STATE)
    dt = jax.nn.softplus(dt_raw.astype(jnp.float32).reshape(b, L, 2, M_HEADS) + dt_bias.astype(jnp.float32))
    a = -jnp.exp(a_log.astype(jnp.float32))
    return xs, Bm, Cm, dt, a


def bidir_ssd(xs, Bm, Cm, dt, a, init_f, init_b, with_output):
    flip = lambda t: jnp.flip(t, axis=1)
    y_f, fin_f = ssd_scan(xs, dt[:, :, 0], a[0], Bm, Cm, init_f, with_output)
    y_b, fin_b = ssd_scan(flip(xs), flip(dt[:, :, 1]), a[1], flip(Bm), flip(Cm), init_b, with_output)
    if not with_output:
        return None, fin_f, fin_b
    return y_f + flip(y_b), fin_f, fin_b


def mamba_output(y, xs, z, d_skip, norm_g):
    b, L = z.shape[:2]
    y = (y + xs * d_skip[:, None]).reshape(b, L, M_D_INNER) * jax.nn.silu(z)
    y = rmsnorm(y.reshape(b, L, M_GROUPS, M_D_INNER // M_GROUPS), norm_g.reshape(M_GROUPS, -1))
    return y.reshape(b, L, M_D_INNER)


def hyena_filters(L, w1, b1, w2, b2, freq, w3):
    f32 = jnp.float32
    t = jnp.linspace(0.0, 1.0, L, dtype=f32)[:, None]
    ang = (2.0 * math.pi / L) * jnp.arange(L, dtype=f32)[:, None]
    bands = jnp.linspace(1e-4, H_BANDS - 1, H_BANDS, dtype=f32)[None, :]
    feats = jnp.concatenate([t, jnp.cos(bands * ang), -jnp.sin(bands * ang)], axis=-1)
    fr = freq.astype(f32)
    hid = jnp.sin(fr * (feats @ w1.astype(f32) + b1.astype(f32)))
    hid = jnp.sin(fr * (hid @ w2.astype(f32) + b2.astype(f32)))
    filt = hid @ w3.astype(f32)
    deltas = jnp.abs(jnp.linspace(math.log(H_DECAY_TARGET) / H_SLOW_PCT,
                                  math.log(H_DECAY_TARGET) / H_FAST_PCT, H_WIDTH, dtype=f32))
    filt = filt * jnp.exp(-t * jnp.tile(deltas, 2)[None, :])
    f_fwd, f_bwd = filt[:, :H_WIDTH], filt[:, H_WIDTH:]
    full = jnp.concatenate([f_fwd, jnp.zeros((1, H_WIDTH), f32), jnp.flip(f_bwd[1:], axis=0)], axis=0)
    return full * lax.rsqrt(jnp.sum(full * full, axis=0, keepdims=True) + RMS_EPS)


def fft_long_conv(u, filt, bias):
    L = u.shape[1]
    n = 2 * L
    uf = jnp.fft.rfft(u.astype(jnp.float32), n=n, axis=1)
    ff = jnp.fft.rfft(filt, n=n, axis=0)
    y = jnp.fft.irfft(uf * ff[None], n=n, axis=1)[:, :L]
    return (y + u.astype(jnp.float32) * bias.astype(jnp.float32)).astype(u.dtype)


def hyena_branch(proj, conv_w, conv_b, w1, b1, w2, b2, freq, w3, bias):
    uc = centred_dwconv(proj, conv_w, conv_b)
    x0, x1, v = jnp.split(uc, 3, axis=-1)
    filt = hyena_filters(proj.shape[1], w1, b1, w2, b2, freq, w3)
    return x0 * fft_long_conv(x1 * v, filt, bias)


def mix_sublayer(x, mods, lp, init_f, init_b):
    h = modulated_norm(x, lp['norm1_g'], mods[0], mods[1])
    p = h @ lp['w_in']
    xs, Bm, Cm, dt, a = mamba_inputs(p[..., :OFF_DT], p[..., OFF_DT:OFF_Z], lp['m_conv_w'], lp['m_conv_b'],
                                     lp['m_dt_bias'], lp['m_a_log'])
    y_ssd, fin_f, fin_b = bidir_ssd(xs, Bm, Cm, dt, a, init_f, init_b, True)
    y_m = mamba_output(y_ssd, xs, p[..., OFF_Z:OFF_HY], lp['m_d'], lp['m_norm_g'])
    y_h = hyena_branch(p[..., OFF_HY:OFF_GATE], lp['h_conv_w'], lp['h_conv_b'], lp['hf_w1'], lp['hf_b1'],
                       lp['hf_w2'], lp['hf_b2'], lp['hf_freq'], lp['hf_w3'], lp['h_bias'])
    gates = jax.nn.sigmoid(p[..., OFF_GATE:])
    merged = gates[..., :D_MODEL] * (y_m @ lp['m_w_out']) + gates[..., D_MODEL:] * (y_h @ lp['h_w_out'])
    return x + mods[2] * (merged @ lp['w_merge_out']), fin_f, fin_b


def ctx_final_states(x, mods, lp, init_f, init_b):
    h = modulated_norm(x, lp['norm1_g'], mods[0], mods[1])
    p = h @ lp['w_in'][:, :OFF_Z]
    xs, Bm, Cm, dt, a = mamba_inputs(p[..., :OFF_DT], p[..., OFF_DT:], lp['m_conv_w'], lp['m_conv_b'],
                                     lp['m_dt_bias'], lp['m_a_log'])
    _, fin_f, fin_b = bidir_ssd(xs, Bm, Cm, dt, a, init_f, init_b, False)
    return fin_f, fin_b


def ffn_sublayer(x, mods, lp):
    h = modulated_norm(x, lp['norm2_g'], mods[3], mods[4])
    g, u = jnp.split(h @ lp['ffn_w_gu'], 2, axis=-1)
    return x + mods[5] * ((jax.nn.silu(g) * u) @ lp['ffn_w_down'])


def setup_inputs(seed: int = 0) -> dict:
    key = jax.random.key(seed)
    ks = jax.random.split(key, 32)
    f32 = jnp.float32

    def nrm(k, shape, scale):
        return jax.random.normal(k, shape, f32) * scale

    dt0 = jnp.exp(jax.random.uniform(ks[8], (DEPTH, 2, M_HEADS), f32, math.log(M_DT_MIN), math.log(M_DT_MAX)))
    return {
        'x': nrm(ks[0], (BATCH, SEQ, D_MODEL), 1.0),
        'c': nrm(ks[1], (BATCH, D_MODEL), 1.0),
        'ctx': nrm(ks[2], (BATCH, CTX_LEN, D_MODEL), 1.0),
        'c_ctx': nrm(ks[3], (D_MODEL,), 1.0),
        'ada_w': nrm(ks[4], (DEPTH, D_MODEL, ADA_CHUNKS * D_MODEL), 0.5 * D_MODEL ** -0.5),
        'ada_b': nrm(ks[5], (DEPTH, ADA_CHUNKS * D_MODEL), 0.02),
        'norm1_g': 1.0 + nrm(ks[6], (DEPTH, D_MODEL), 0.05),
        'w_in': nrm(ks[7], (DEPTH, D_MODEL, IN_COLS), D_MODEL ** -0.5),
        'm_conv_w': nrm(ks[9], (DEPTH, M_CONV, M_XBC), M_CONV ** -0.5),
        'm_conv_b': nrm(ks[10], (DEPTH, M_XBC), 0.02),
        'm_dt_bias': dt0 + jnp.log(-jnp.expm1(-dt0)),
        'm_a_log': jnp.log(jax.random.uniform(ks[11], (DEPTH, 2, M_HEADS), f32, 1.0, 16.0)),
        'm_d': 1.0 + nrm(ks[12], (DEPTH, M_HEADS), 0.1),
        'm_norm_g': 1.0 + nrm(ks[13], (DEPTH, M_D_INNER), 0.05),
        'm_w_out': nrm(ks[14], (DEPTH, M_D_INNER, D_MODEL), M_D_INNER ** -0.5),
        'h_conv_w': nrm(ks[15], (DEPTH, H_SHORT, 3 * H_WIDTH), H_SHORT ** -0.5),
        'h_conv_b': nrm(ks[16], (DEPTH, 3 * H_WIDTH), 0.02),
        'hf_w1': nrm(ks[17], (DEPTH, H_EMB, H_HIDDEN), H_EMB ** -0.5),
        'hf_b1': nrm(ks[18], (DEPTH, H_HIDDEN), 0.02),
        'hf_w2': nrm(ks[19], (DEPTH, H_HIDDEN, H_HIDDEN), H_HIDDEN ** -0.5),
        'hf_b2': nrm(ks[20], (DEPTH, H_HIDDEN), 0.02),
        'hf_freq': 1.0 + nrm(ks[21], (DEPTH, H_HIDDEN), 0.1),
        'hf_w3': nrm(ks[22], (DEPTH, H_HIDDEN, 2 * H_WIDTH), H_HIDDEN ** -0.5),
        'h_bias': nrm(ks[23], (DEPTH, H_WIDTH), 0.1),
        'h_w_out': nrm(ks[24], (DEPTH, H_WIDTH, D_MODEL), H_WIDTH ** -0.5),
        'w_merge_out': nrm(ks[25], (DEPTH, D_MODEL, D_MODEL), D_MODEL ** -0.5),
        'norm2_g': 1.0 + nrm(ks[26], (DEPTH, D_MODEL), 0.05),
        'ffn_w_gu': nrm(ks[27], (DEPTH, D_MODEL, 2 * FFN_HIDDEN), D_MODEL ** -0.5),
        'ffn_w_down': nrm(ks[28], (DEPTH, FFN_HIDDEN, D_MODEL), FFN_HIDDEN ** -0.5),
        'final_g': 1.0 + nrm(ks[29], (D_MODEL,), 0.05),
    }


def reference(x, c, ctx, c_ctx, ada_w, ada_b, norm1_g, w_in, m_conv_w, m_conv_b, m_dt_bias, m_a_log, m_d,
              m_norm_g, m_w_out, h_conv_w, h_conv_b, hf_w1, hf_b1, hf_w2, hf_b2, hf_freq, hf_w3, h_bias,
              h_w_out, w_merge_out, norm2_g, ffn_w_gu, ffn_w_down, final_g):
    b, L, _ = x.shape
    rows = L // GRID_W
    x_l = x + grid_pos_embed(rows, D_MODEL, x.dtype)[None]
    x_c = ctx
    silu_c = jax.nn.silu(c)[:, None, :]
    silu_cc = jax.nn.silu(c_ctx)
    zero_state = jnp.zeros((b, M_GROUPS, M_HEADS // M_GROUPS, M_HEADDIM, M_STATE), jnp.float32)
    for i in range(DEPTH):
        lp = dict(w_in=w_in[i], norm1_g=norm1_g[i], m_conv_w=m_conv_w[i], m_conv_b=m_conv_b[i],
                  m_dt_bias=m_dt_bias[i], m_a_log=m_a_log[i], m_d=m_d[i], m_norm_g=m_norm_g[i],
                  m_w_out=m_w_out[i], h_conv_w=h_conv_w[i], h_conv_b=h_conv_b[i], hf_w1=hf_w1[i],
                  hf_b1=hf_b1[i], hf_w2=hf_w2[i], hf_b2=hf_b2[i], hf_freq=hf_freq[i], hf_w3=hf_w3[i],
                  h_bias=h_bias[i], h_w_out=h_w_out[i], w_merge_out=w_merge_out[i], norm2_g=norm2_g[i],
                  ffn_w_gu=ffn_w_gu[i], ffn_w_down=ffn_w_down[i])
        mods_l = jnp.split(silu_c @ ada_w[i] + ada_b[i], ADA_CHUNKS, axis=-1)
        mods_c = jnp.split(silu_cc @ ada_w[i] + ada_b[i], ADA_CHUNKS, axis=-1)
        if i < DEPTH - 1:
            x_c_mixed, fin_f, fin_b = mix_sublayer(x_c, mods_c, lp, zero_state, zero_state)
        else:
            fin_f, fin_b = ctx_final_states(x_c, mods_c, lp, zero_state, zero_state)
        x_l, _, _ = mix_sublayer(x_l, mods_l, lp, fin_f, fin_b)
        x_l = ffn_sublayer(x_l, mods_l, lp)
        if i < DEPTH - 1:
            x_c = ffn_sublayer(x_c_mixed, mods_c, lp)
    return rmsnorm(x_l, final_g)
```

```python
import contextlib
import math
import numpy as np
import concourse.bass as bass
import concourse.mybir as mybir
from concourse.bass_utils import run_bass_kernel_spmd

F32 = mybir.dt.float32
F32R = mybir.dt.float32r
BF16 = mybir.dt.bfloat16
I32 = mybir.dt.int32
ALU = mybir.AluOpType
AF = mybir.ActivationFunctionType
AX = mybir.AxisListType

COMPUTE = ("pe", "act", "dve", "pool")
QUEUES = ("sp",) + COMPUTE


class _Op:
    __slots__ = ("eng", "fn", "deps", "signal", "dma_key", "dma_val", "sigval")

    def __init__(self, eng, fn):
        self.eng = eng
        self.fn = fn
        self.deps = []
        self.signal = False
        self.dma_key = None
        self.dma_val = 0
        self.sigval = 0


class Sched:
    def __init__(self, nc):
        self.nc = nc
        self.stack = contextlib.ExitStack()
        self.ops = {e: [] for e in QUEUES}
        self.last_w = {}
        self.readers = {}
        self.sems = {}
        self.sig_count = {e: 0 for e in COMPUTE}
        self.dma_count = {}
        self.dma_last = {}
        self.waited = {e: {} for e in QUEUES}
        self.n_tiles = 0
        self.n_emitted = 0
        self.slot_of = {}

    def sem(self, key):
        if key not in self.sems:
            self.sems[key] = self.stack.enter_context(self.nc.semaphore("s%d" % len(self.sems)))
        return self.sems[key]

    def sbuf(self, shape, dtype, name="t", stack=None):
        self.n_tiles += 1
        return (stack or self.stack).enter_context(
            self.nc.sbuf_tensor("%s_%d" % (name, self.n_tiles), list(shape), dtype))

    def psum(self, shape, dtype, name="p", stack=None):
        self.n_tiles += 1
        return (stack or self.stack).enter_context(
            self.nc.psum_tensor("%s_%d" % (name, self.n_tiles), list(shape), dtype))

    @staticmethod
    def _k(r):
        if isinstance(r, tuple):
            return tuple(Sched._k(x) for x in r)
        if isinstance(r, (str, int)):
            return r
        return ("id", id(r))

    def _add(self, eng, fn, reads, writes):
        op = _Op(eng, fn)
        deps = []
        for r in reads:
            w = self.last_w.get(r)
            if w is not None:
                deps.append(w)
        for r in writes:
            w = self.last_w.get(r)
            if w is not None:
                deps.append(w)
            deps.extend(self.readers.get(r, ()))
        seen = set()
        for d in deps:
            if d is op or id(d) in seen:
                continue
            seen.add(id(d))
            if d.eng == eng and d.dma_key is None and eng in ("pe", "sp"):
                continue
            op.deps.append(d)
        for r in writes:
            self.last_w[r] = op
            self.readers[r] = []
        for r in reads:
            if r not in writes:
                self.readers.setdefault(r, []).append(op)
        self.ops[eng].append(op)
        return op

    def op(self, eng, fn, reads=(), writes=()):
        return self._add(eng, fn, tuple(self._k(r) for r in reads), tuple(self._k(r) for r in writes))

    def dma(self, out, in_, key, reads=(), writes=(), queue="sp"):
        def fn(e, out=out, in_=in_):
            return e.dma_start(out=out, in_=in_)
        op = self._add(queue, fn, tuple(self._k(r) for r in reads), tuple(self._k(r) for r in writes))
        key = self._k(key)
        if key not in self.slot_of:
            self.slot_of[key] = len(self.slot_of)
        key = self.slot_of[key]
        prev = self.dma_last.get(key)
        if prev is not None and all(d is not prev for d in op.deps):
            op.deps.append(prev)
        op.dma_key = key
        self.dma_count[key] = self.dma_count.get(key, 0) + 16
        op.dma_val = self.dma_count[key]
        self.dma_last[key] = op
        self.sem(("dma", key))
        return op

    def flush(self):
        if self.dma_last:
            fin = self._add("sp", None, (), ())
            for d in self.dma_last.values():
                if all(x is not d for x in fin.deps):
                    fin.deps.append(d)
        for e in QUEUES:
            for op in self.ops[e]:
                for d in op.deps:
                    if d.dma_key is None:
                        d.signal = True
        for e in COMPUTE:
            for op in self.ops[e]:
                if op.dma_key is None and op.signal:
                    self.sig_count[e] += 1
                    op.sigval = self.sig_count[e]
            self.sem(("eng", e))
        handles = {"sp": "sync", "pe": "tensor", "act": "scalar", "dve": "vector", "pool": "gpsimd"}
        with self.nc.Block() as block:
            for e in QUEUES:
                todo = self.ops[e]
                if not todo:
                    continue

                def body(eng, e=e, todo=todo):
                    waited = self.waited[e]
                    for op in todo:
                        for d in op.deps:
                            if d.dma_key is not None:
                                k, v = ("dma", d.dma_key), d.dma_val
                            else:
                                k, v = ("eng", d.eng), d.sigval
                            if waited.get(k, 0) >= v:
                                continue
                            waited[k] = v
                            eng.wait_ge(self.sems[k], v)
                        if op.fn is None:
                            continue
                        inst = op.fn(eng)
                        if op.dma_key is not None:
                            inst.then_inc(self.sems[("dma", op.dma_key)], 16)
                        elif op.signal:
                            inst.then_inc(self.sems[("eng", e)], 1)

                getattr(block, handles[e])(body)
                self.n_emitted += len(todo)
        self.ops = {e: [] for e in QUEUES}
        self.last_w.clear()
        self.readers.clear()
        self.dma_last.clear()
        self.slot_of = {}

    def tt(self, eng, out, a, b, op, r, w):
        self.op(eng, lambda e: e.tensor_tensor(out=out, in0=a, in1=b, op=op), r, w)

    def ts(self, eng, out, a, s1, s2, op0, op1, r, w):
        if s2 is None:
            self.op(eng, lambda e: e.tensor_scalar(out=out, in0=a, scalar1=s1, scalar2=None, op0=op0), r, w)
        else:
            self.op(eng, lambda e: e.tensor_scalar(out=out, in0=a, scalar1=s1, scalar2=s2, op0=op0, op1=op1), r, w)

    def stt(self, eng, out, a, sc, b, op0, op1, r, w):
        self.op(eng, lambda e: e.scalar_tensor_tensor(out=out, in0=a, scalar=sc, in1=b, op0=op0, op1=op1), r, w)

    def act(self, out, in_, func, r, w, bias=None, scale=None, accum=None):
        kw = {}
        if bias is not None:
            kw["bias"] = bias
        if scale is not None:
            kw["scale"] = scale
        if accum is not None:
            kw["accum_out"] = accum
        self.op("act", lambda e: e.activation(out=out, in_=in_, func=func, **kw), r, w)

    def cp(self, eng, out, in_, r, w):
        if eng == "act":
            self.op("act", lambda e: e.copy(out=out, in_=in_), r, w)
        else:
            self.op(eng, lambda e: e.tensor_copy(out=out, in_=in_), r, w)

    def mm(self, out, lhsT, rhs, start, stop, r, w):
        self.op("pe", lambda e: e.matmul(out, lhsT=lhsT, rhs=rhs, start=start, stop=stop), r, w)

    def tr(self, out, in_, ident, r, w):
        self.op("pe", lambda e: e.transpose(out, in_, ident), r, w)

    def memset(self, eng, ap, val, w):
        self.op(eng, lambda e: e.memset(ap, val), (), w)


class Cfg:
    def __init__(s, D=2048, L=8192, LC=256, DEPTH=4, B=4):
        s.D, s.L, s.LC, s.DEPTH, s.B = D, L, LC, DEPTH, B
        s.KD = D // 128
        s.DI = 2 * D
        s.H = s.DI // 64
        s.G = 8
        s.R = s.H // s.G
        s.N = 128
        s.GN = s.G * s.N
        s.XBC = s.DI + 2 * s.GN
        s.FF = -(-8 * D // (3 * 256)) * 256
        s.OFF_DT = s.XBC
        s.OFF_Z = s.OFF_DT + 2 * s.H
        s.OFF_HY = s.OFF_Z + s.DI
        s.OFF_GATE = s.OFF_HY + 3 * D
        s.IN_COLS = s.OFF_GATE + 2 * D
        s.TT = LC + L
        s.NFM = s.XBC + 3 * D + 2 * D
        s.NTM = s.DI + 2 * s.H
        s.EPS = 1e-6
        s.TB = 1024 if L >= 1024 else L
        s.TF = 512
        s.HB = min(4, s.R)

    def blocks(s, T):
        out = [(0, s.LC, 1)]
        t = s.LC
        while t < s.TT:
            out.append((t, min(T, s.TT - t), 0))
            t += T
        return out


def _tile_w(W, cw):
    K, C = W.shape
    assert K % 128 == 0 and C % cw == 0
    return np.ascontiguousarray(W.reshape(K // 128, 128, C // cw, cw).transpose(2, 1, 0, 3)).astype(np.float32)


def _cols(v, n=128):
    return np.ascontiguousarray(v.reshape(-1, n).T).astype(np.float32)


def _rep(v):
    return np.ascontiguousarray(np.broadcast_to(np.asarray(v, np.float32).reshape(1, -1), (128, v.size)))


def _pos_embed(cfg, grid_w=64, base=10000.0):
    rows = cfg.L // grid_w
    r, col = np.meshgrid(np.arange(rows), np.arange(grid_w), indexing="ij")
    quarter = cfg.D // 4
    omega = (1.0 / (np.float32(base) ** (np.arange(quarter, dtype=np.float32) / np.float32(quarter)))).astype(np.float32)

    def ax(pos):
        ang = pos.reshape(-1)[:, None].astype(np.float32) * omega[None, :]
        return np.concatenate([np.sin(ang), np.cos(ang)], axis=-1)

    return np.concatenate([ax(r), ax(col)], axis=-1).astype(np.float32)


def _fft_tables(N1):
    N = N1 * 128
    P = 128 // N1
    i1 = np.arange(N1)
    i2 = np.arange(128)
    a1 = 2 * np.pi * np.outer(i1, i1) / N1
    eye = np.eye(P)
    F1bd = np.stack([np.kron(eye, np.cos(a1)), np.kron(eye, -np.sin(a1))], axis=1)
    at = 2 * np.pi * np.outer(i2, i1) / N
    Tw1 = np.stack([np.tile(np.cos(at), (1, P)), np.tile(-np.sin(at), (1, P))], axis=1)
    a2 = 2 * np.pi * np.outer(i2, i2) / 128
    F2 = np.stack([np.cos(a2), -np.sin(a2), np.sin(a2)], axis=1)
    Finv2 = np.stack([np.cos(a2), np.sin(a2), -np.sin(a2), np.cos(a2)], axis=1)
    atb = 2 * np.pi * np.outer(i1, i2) / N
    Tw2 = np.stack([np.tile(np.cos(atb), (P, 1)), np.tile(np.sin(atb), (P, 1))], axis=1)
    Finv1bd = np.stack([np.kron(eye, np.cos(a1) / N), np.kron(eye, -np.sin(a1) / N)], axis=1)
    f = lambda x: np.ascontiguousarray(x).astype(np.float32)
    return dict(F1bd=f(F1bd), Tw1=f(Tw1), F2=f(F2), Finv2=f(Finv2), Tw2=f(Tw2), Finv1bd=f(Finv1bd))


def _filter_tables(Lx, D, emb=33):
    f32 = np.float32
    bands_n = (emb - 1) // 2
    t = np.linspace(0.0, 1.0, Lx, dtype=f32)
    ang = (f32(2.0 * math.pi / Lx) * np.arange(Lx, dtype=f32))
    bands = np.linspace(1e-4, bands_n - 1, bands_n, dtype=f32)[None, :]
    feats = np.concatenate([t[:, None], np.cos(bands * ang[:, None]), -np.sin(bands * ang[:, None])], axis=-1).astype(f32)
    idx = np.concatenate([np.arange(Lx), [0], np.arange(Lx - 1, 0, -1)])
    featsT = np.ascontiguousarray(feats[idx].T)
    tpos = t[idx].copy()
    tpos[Lx] = 1e4
    return featsT.astype(f32), _rep(tpos)


def _deltas(D):
    f32 = np.float32
    return np.abs(np.linspace(math.log(1e-2) / 1.5, math.log(1e-2) / 0.3, D, dtype=f32)).astype(f32)


def host_prepare(cfg, inp, b):
    D, KD, DEPTH = cfg.D, cfg.KD, cfg.DEPTH
    f32 = np.float32
    m = {}
    xT = np.concatenate([inp["ctx"][b].T, inp["x"][b].T], axis=1)
    m["xin"] = np.ascontiguousarray(xT).astype(f32)
    pos = np.concatenate([np.zeros((cfg.LC, D), f32), _pos_embed(cfg)], axis=0)
    m["pos"] = np.ascontiguousarray(pos.T)
    cc = np.stack([inp["c"][b], inp["c_ctx"]], axis=1)
    m["cT"] = np.ascontiguousarray(cc.reshape(KD, 128, 2).transpose(1, 0, 2)).astype(f32)
    m["ada_w"] = np.stack([_tile_w(inp["ada_w"][l], 128) for l in range(DEPTH)])
    m["ada_b"] = np.stack([_cols(inp["ada_b"][l]) for l in range(DEPTH)])
    m["g1"] = np.stack([_cols(inp["norm1_g"][l]) for l in range(DEPTH)])
    m["g2"] = np.stack([_cols(inp["norm2_g"][l]) for l in range(DEPTH)])
    m["gfin"] = _cols(inp["final_g"])
    wfm, wtm = [], []
    for l in range(DEPTH):
        W = inp["w_in"][l]
        fm = np.concatenate([W[:, :cfg.XBC], W[:, cfg.OFF_HY:cfg.OFF_GATE], W[:, cfg.OFF_GATE:]], axis=1)
        wfm.append(_tile_w(fm, 128))
        wtm.append(np.concatenate([W[:, cfg.OFF_Z:cfg.OFF_HY], W[:, cfg.OFF_DT:cfg.OFF_Z]], axis=1))
    m["w_fm"] = np.stack(wfm)
    tmb = []
    for l in range(DEPTH):
        W = wtm[l]
        blks = []
        c = 0
        while c < cfg.NTM:
            cw = min(512, cfg.DI - c) if c < cfg.DI else cfg.NTM - c
            blk = np.zeros((D, 512), f32)
            blk[:, :cw] = W[:, c:c + cw]
            blks.append(_tile_w(blk, 512)[0])
            c += cw
        tmb.append(np.stack(blks))
    m["w_tm"] = np.stack(tmb)
    m["mcw"] = np.stack([np.ascontiguousarray(inp["m_conv_w"][l].T.reshape(-1, 128, 5).transpose(1, 0, 2)) for l in range(DEPTH)]).astype(f32)
    m["mcb"] = np.stack([_cols(inp["m_conv_b"][l]) for l in range(DEPTH)])
    m["hcw"] = np.stack([np.ascontiguousarray(inp["h_conv_w"][l].T.reshape(-1, 128, 3).transpose(1, 0, 2)) for l in range(DEPTH)]).astype(f32)
    m["hcb"] = np.stack([_cols(inp["h_conv_b"][l]) for l in range(DEPTH)])
    m["dtb"] = np.stack([_rep(inp["m_dt_bias"][l].reshape(-1)) for l in range(DEPTH)])
    m["alog"] = np.stack([_rep(inp["m_a_log"][l].reshape(-1)) for l in range(DEPTH)])
    m["dskip"] = np.stack([_rep(inp["m_d"][l]) for l in range(DEPTH)])
    m["mng"] = np.stack([_cols(inp["m_norm_g"][l]) for l in range(DEPTH)])
    m["mwo"] = np.stack([_tile_w(inp["m_w_out"][l], 128) for l in range(DEPTH)])
    m["hwo"] = np.stack([_tile_w(inp["h_w_out"][l], 128) for l in range(DEPTH)])
    m["wmo"] = np.stack([_tile_w(inp["w_merge_out"][l], 128) for l in range(DEPTH)])
    m["wg"] = np.stack([_tile_w(inp["ffn_w_gu"][l][:, :cfg.FF], 128) for l in range(DEPTH)])
    m["wu"] = np.stack([_tile_w(inp["ffn_w_gu"][l][:, cfg.FF:], 128) for l in range(DEPTH)])
    m["wd"] = np.stack([_tile_w(inp["ffn_w_down"][l], 128) for l in range(DEPTH)])
    m["hw1"] = np.ascontiguousarray(inp["hf_w1"]).astype(f32)
    m["hw2"] = np.ascontiguousarray(inp["hf_w2"]).astype(f32)
    m["hw3"] = np.ascontiguousarray(inp["hf_w3"]).astype(f32)
    m["hb1"] = np.ascontiguousarray(inp["hf_b1"][:, :, None]).astype(f32)
    m["hb2"] = np.ascontiguousarray(inp["hf_b2"][:, :, None]).astype(f32)
    m["hfr"] = np.ascontiguousarray(inp["hf_freq"][:, :, None]).astype(f32)
    for nm, Lx in (("L", cfg.L), ("C", cfg.LC)):
        N1 = 2 * Lx // 128
        P = 128 // N1
        for k, v in _fft_tables(N1).items():
            m[k + nm] = v
        ft, tp = _filter_tables(Lx, D)
        m["feats" + nm] = ft
        m["tpos" + nm] = tp
        hb = np.stack([np.repeat(inp["h_bias"][l].reshape(-1, P), N1, axis=1).T for l in range(DEPTH)])
        m["hbias" + nm] = np.ascontiguousarray(hb).astype(f32)
    m["ndelta"] = _cols(-_deltas(D))
    i = np.arange(128)
    Uf = (i[:, None] <= i[None, :]).astype(f32)
    Lf = (i[:, None] > i[None, :]).astype(f32)
    Ub = (i[:, None] >= i[None, :]).astype(f32)
    Lb = (i[:, None] < i[None, :]).astype(f32)
    m["masks"] = np.ascontiguousarray(np.stack([Uf, Lf, Ub, Lb, np.ones((128, 128), f32), np.eye(128, dtype=f32)], axis=1))
    return m


class Prog:
    def __init__(self, cfg, shapes):
        self.cfg = cfg
        self.nc = bass.Bass("TRN2", target_bir_lowering=False)
        self.S = Sched(self.nc)
        nc = self.nc
        self.din = {k: nc.dram_tensor(k, list(shp), F32, kind="ExternalInput").ap() for k, shp in shapes.items()}
        c = cfg
        self.out = nc.dram_tensor("out", [c.D, c.L], F32, kind="ExternalOutput").ap()

        def scr(name, shape):
            return nc.dram_tensor(name, list(shape), F32, kind="Internal").ap()

        self.xT = scr("xT", [c.D, c.TT])
        self.PTR = 4096 if c.NFM > 4096 else c.NFM
        self.pT_parts = [scr("pT%d" % i, [min(self.PTR, c.NFM - i * self.PTR), c.TT]) for i in range(-(-c.NFM // self.PTR))]
        self.z_tm = scr("z_tm", [c.TT, c.DI])
        self.dt_tm = scr("dt_tm", [c.TT, 2 * c.H])
        self.xs_tm = scr("xs_tm", [c.TT, c.DI])
        self.B_tm = scr("B_tm", [c.TT, c.GN])
        self.B_fm = scr("B_fm", [c.GN, c.TT])
        self.C_fm = scr("C_fm", [c.GN, c.TT])
        self.yf_tm = scr("yf_tm", [c.TT, c.DI])
        self.ym_fm = scr("ym_fm", [c.DI, c.TT])
        self.hy = {}
        for nm, Lx in (("L", c.L), ("C", c.LC)):
            self.hy[nm] = dict(
                L=Lx, N1=2 * Lx // 128, P=128 // (2 * Lx // 128),
                x0=scr("x0" + nm, [c.D, 2 * Lx]), u=scr("u" + nm, [c.D, 2 * Lx]),
                yh=scr("yh" + nm, [c.D, 2 * Lx]), filt=scr("filt" + nm, [c.D, 2 * Lx]),
                HP=min(512, c.D * (2 * Lx // 128) // 128),
                H=[scr("H%s%d" % (nm, i), [min(512, c.D * (2 * Lx // 128) // 128), 128, 256])
                   for i in range(-(-(c.D * (2 * Lx // 128) // 128) // 512))])
        S = self.S
        self.masks = S.sbuf([128, 6, 128], F32, "masks")
        self.masksr = S.sbuf([128, 6, 128], F32R, "masksr")
        self.mods = [S.sbuf([128, 6 * c.KD, 2], F32, "mods") for _ in range(c.DEPTH)]
        self.A1 = [S.sbuf([128, c.KD, 2], F32, "A1") for _ in range(c.DEPTH)]
        self.A2 = [S.sbuf([128, c.KD, 2], F32, "A2") for _ in range(c.DEPTH)]
        self.cpi = S.sbuf([128, 1], F32, "cpi")

    def pTc(self, cc):
        r0 = cc * 128
        part = self.pT_parts[r0 // self.PTR]
        r1 = r0 % self.PTR
        return part[r1:r1 + 128, :]

    def Hc(self, nm, ps):
        hh = self.hy[nm]
        return hh["H"][ps // hh["HP"]][ps % hh["HP"]]

    def build(self, nstages=None):
        c = self.cfg
        stages = [self.prologue]
        for l in range(c.DEPTH):
            stages += [lambda l=l: self.in_proj(l), lambda l=l: self.conv_stage(l),
                       lambda l=l: self.ssd(l, 0), lambda l=l: self.ssd(l, 1)]
            for nm in ("L", "C"):
                stages += [lambda l=l, nm=nm: self.filter_gen(l, nm),
                           lambda l=l, nm=nm: self.fft_pass(l, nm, True),
                           lambda l=l, nm=nm: self.fft_pass(l, nm, False)]
            stages += [lambda l=l: self.out_proj(l), lambda l=l: self.ffn(l)]
        stages.append(self.final)
        for f in stages[:nstages]:
            f()
        return self.nc

    def prologue(self):
        c, S, din = self.cfg, self.S, self.din
        st = contextlib.ExitStack()
        S.dma(self.masks[:], din["masks"], key="masks", writes=[self.masks])
        S.dma(self.masksr[:], din["masks"], key="masksr", writes=[self.masksr], queue="pool")
        S.memset("dve", self.cpi[:], -math.pi, [self.cpi])
        xa = [S.sbuf([128, c.TT], F32, "xa", st) for _ in range(2)]
        xb = [S.sbuf([128, c.TT], F32, "xb", st) for _ in range(2)]
        for k in range(c.KD):
            a, b_ = xa[k % 2], xb[k % 2]
            S.dma(a[:], din["xin"][k * 128:(k + 1) * 128, :], key=("xa", k % 2), writes=[a])
            S.dma(b_[:], din["pos"][k * 128:(k + 1) * 128, :], key=("xb", k % 2), writes=[b_])
            S.tt("dve", a[:], a[:], b_[:], ALU.add, [a, b_], [a])
            S.dma(self.xT[k * 128:(k + 1) * 128, :], a[:], key=("xa", k % 2), reads=[a])
        for nm in ("L", "C"):
            h = self.hy[nm]
            Lx = h["L"]
            S.memset("pool", xb[0][:, 0:Lx], 0.0, [xb[0]])
            for k in range(c.KD):
                for t in (h["x0"], h["u"]):
                    S.dma(t[k * 128:(k + 1) * 128, Lx:2 * Lx], xb[0][:, 0:Lx], key="zpad", reads=[xb[0]])
        S.flush()
        st.close()
        st = contextlib.ExitStack()
        sc = S.sbuf([128, c.KD, 2], F32, "sc", st)
        S.dma(sc[:], din["cT"], key="sc", writes=[sc])
        S.act(sc[:], sc[:], AF.Silu, [sc], [sc])
        wt = [S.sbuf([128, c.KD, 128], F32, "adaw", st) for _ in range(3)]
        pm = S.psum([128, 6 * c.KD, 2], F32, "pm", st)
        ab = S.sbuf([128, 6 * c.KD], F32, "ab", st)
        g = S.sbuf([128, c.KD], F32, "g", st)
        for l in range(c.DEPTH):
            for cc in range(6 * c.KD):
                w = wt[cc % 3]
                S.dma(w[:], din["ada_w"][l, cc], key=("adaw", cc % 3), writes=[w])
                for k in range(c.KD):
                    S.mm(pm[:, cc, :], w[:, k, :], sc[:, k, :], k == 0, k == c.KD - 1, [w, sc], [pm])
            S.dma(ab[:], din["ada_b"][l], key="ab", writes=[ab])
            md = self.mods[l]
            S.tt("dve", md[:], pm[:], ab[:].unsqueeze(2).to_broadcast([128, 6 * c.KD, 2]), ALU.add, [pm, ab], [md])
            for (A, gname, off) in ((self.A1[l], "g1", c.KD), (self.A2[l], "g2", 4 * c.KD)):
                S.dma(g[:], din[gname][l], key="g", writes=[g])
                S.ts("dve", A[:], md[:, off:off + c.KD, :], 1.0, None, ALU.add, None, [md], [A])
                S.tt("dve", A[:], A[:], g[:].unsqueeze(2).to_broadcast([128, c.KD, 2]), ALU.mult, [A, g], [A])
        S.flush()
        st.close()

    def norm_block(self, st, T, names="n", hdt=F32R):
        c, S = self.cfg, self.S
        xk = [S.sbuf([128, T], F32, "xk" + names, st) for _ in range(3)]
        h = S.sbuf([128, c.KD, T], hdt, "h" + names, st)
        sq = [S.sbuf([128, T], F32R, "sq" + names, st) for _ in range(2)]
        rstd = S.sbuf([128, T], F32, "rstd" + names, st)
        tmp = [S.sbuf([128, T], F32, "ntmp" + names, st) for _ in range(2)]
        nb = (T + 511) // 512
        pst = [S.psum([128, 512], F32, "pst" + names, st) for _ in range(nb)]
        return dict(xk=xk, h=h, sq=sq, rstd=rstd, tmp=tmp, pst=pst, T=T, n=names)

    def norm_run(self, nb_, t0, T, r, A, Bsh, boff):
        c, S = self.cfg, self.S
        xk, h, sq, rstd, tmp, pst = nb_["xk"], nb_["h"], nb_["sq"], nb_["rstd"], nb_["tmp"], nb_["pst"]
        ones = self.masksr[:, 4, :]
        it = 0
        for k in range(c.KD):
            x = xk[it % 3]
            S.dma(x[:, :T], self.xT[k * 128:(k + 1) * 128, t0:t0 + T], key=("nx" + nb_["n"], it % 3), writes=[x])
            it += 1
            s = sq[k % 2]
            S.act(s[:, :T], x[:, :T], AF.Square, [x], [s])
            for j in range((T + 511) // 512):
                w = min(512, T - j * 512)
                S.mm(pst[j][:, :w], ones, s[:, j * 512:j * 512 + w], k == 0, k == c.KD - 1, [s, self.masksr], [pst[j]])
        for j in range((T + 511) // 512):
            w = min(512, T - j * 512)
            S.ts("dve", rstd[:, j * 512:j * 512 + w], pst[j][:, :w], 1.0 / c.D, c.EPS, ALU.mult, ALU.add, [pst[j]], [rstd])
        S.act(rstd[:, :T], rstd[:, :T], AF.Sqrt, [rstd], [rstd])
        S.op("dve", lambda e: e.reciprocal(out=rstd[:, :T], in_=rstd[:, :T]), [rstd], [rstd])
        for k in range(c.KD):
            x = xk[it % 3]
            S.dma(x[:, :T], self.xT[k * 128:(k + 1) * 128, t0:t0 + T], key=("nx" + nb_["n"], it % 3), writes=[x])
            it += 1
            t_ = tmp[k % 2]
            S.stt("dve", t_[:, :T], x[:, :T], A[:, k, r:r + 1], rstd[:, :T], ALU.mult, ALU.mult, [x, A, rstd], [t_])
            if Bsh is None:
                S.cp("act", h[:, k, :T], t_[:, :T], [t_], [h])
            else:
                S.act(h[:, k, :T], t_[:, :T], AF.Identity, [t_, Bsh], [h], bias=Bsh[:, boff + k, r:r + 1], scale=1.0)

    def in_proj(self, l):
        c, S, din = self.cfg, self.S, self.din
        st = contextlib.ExitStack()
        T = c.TB
        nbk = self.norm_block(st, T)
        h = nbk["h"]
        wf = [S.sbuf([128, c.KD, 128], F32R, "wf", st) for _ in range(3)]
        wtm = [S.sbuf([128, c.KD, 512], F32R, "wtm", st) for _ in range(2)]
        nb = T // 512 if T >= 512 else 1
        pp = [[S.psum([128, 512], F32, "pp", st) for _ in range(nb)] for _ in range(2)]
        ob = [S.sbuf([128, T], F32, "ob", st) for _ in range(2)]
        otm = [S.sbuf([128, 512], F32, "otm", st) for _ in range(2)]
        ngate0 = (c.XBC + 3 * c.D) // 128
        ntmb = din["w_tm"].shape[1]
        it = 0
        for (t0, Tb, r) in c.blocks(T):
            self.norm_run(nbk, t0, Tb, r, self.A1[l], self.mods[l], 0)
            for cc in range(c.NFM // 128):
                w = wf[cc % 3]
                S.dma(w[:], din["w_fm"][l, cc], key=("wf", cc % 3), writes=[w], queue="pool")
                o = ob[cc % 2]
                for j in range((Tb + 511) // 512):
                    wd = min(512, Tb - j * 512)
                    p = pp[cc % 2][j]
                    for k in range(c.KD):
                        S.mm(p[:, :wd], w[:, k, :], h[:, k, j * 512:j * 512 + wd], k == 0, k == c.KD - 1, [w, h], [p])
                    if cc >= ngate0:
                        S.act(o[:, j * 512:j * 512 + wd], p[:, :wd], AF.Sigmoid, [p], [o])
                    elif (cc + j) % 2 == 0:
                        S.cp("act", o[:, j * 512:j * 512 + wd], p[:, :wd], [p], [o])
                    else:
                        S.cp("dve", o[:, j * 512:j * 512 + wd], p[:, :wd], [p], [o])
                S.dma(self.pTc(cc)[:, t0:t0 + Tb], o[:, :Tb], key=("ob", cc % 2), reads=[o])
            for cb in range(ntmb):
                c0 = cb * 512
                isdt = c0 >= c.DI
                cw = (c.NTM - c.DI) if isdt else min(512, c.DI - c0)
                w = wtm[cb % 2]
                S.dma(w[:], din["w_tm"][l, cb], key=("wtm", cb % 2), writes=[w], queue="pool")
                for tc in range(Tb // 128):
                    p = pp[it % 2][0]
                    o = otm[it % 2]
                    it += 1
                    for k in range(c.KD):
                        S.mm(p[:, :cw], h[:, k, tc * 128:(tc + 1) * 128], w[:, k, :cw], k == 0, k == c.KD - 1, [w, h], [p])
                    if isdt:
                        S.cp("dve", o[:, :cw], p[:, :cw], [p], [o])
                        S.dma(self.dt_tm[t0 + tc * 128:t0 + (tc + 1) * 128, :], o[:, :cw], key=("otm", id(o)), reads=[o])
                    else:
                        S.act(o[:, :cw], p[:, :cw], AF.Silu, [p], [o])
                        S.dma(self.z_tm[t0 + tc * 128:t0 + (tc + 1) * 128, c0:c0 + cw], o[:, :cw], key=("otm", id(o)), reads=[o])
        S.flush()
        st.close()

    def _conv(self, eng, acc, raw, wt, bt, ci, K):
        c, S = self.cfg, self.S
        half = K // 2
        S.ts(eng, acc[:], raw[:], wt[:, ci, half:half + 1], bt[:, ci:ci + 1], ALU.mult, ALU.add, [raw, wt, bt], [acc])
        for j in range(K):
            sh = j - half
            if sh == 0:
                continue
            for (s0, s1) in ((0, c.LC), (c.LC, c.TT)):
                lo, hi = max(s0, s0 - sh), min(s1, s1 - sh)
                S.stt(eng, acc[:, lo:hi], raw[:, lo + sh:hi + sh], wt[:, ci, j:j + 1], acc[:, lo:hi], ALU.mult, ALU.add, [raw, wt, acc], [acc])

    def conv_stage(self, l):
        c, S, din = self.cfg, self.S, self.din
        st = contextlib.ExitStack()
        nxc = c.XBC // 128
        mcw = S.sbuf([128, nxc, 5], F32, "mcw", st)
        mcb = S.sbuf([128, nxc], F32, "mcb", st)
        hcw = S.sbuf([128, 3 * c.KD, 3], F32, "hcw", st)
        hcb = S.sbuf([128, 3 * c.KD], F32, "hcb", st)
        S.dma(mcw[:], din["mcw"][l], key="mcw", writes=[mcw])
        S.dma(mcb[:], din["mcb"][l], key="mcb", writes=[mcb])
        S.dma(hcw[:], din["hcw"][l], key="hcw", writes=[hcw])
        S.dma(hcb[:], din["hcb"][l], key="hcb", writes=[hcb])
        raw = [S.sbuf([128, c.TT], F32, "raw", st) for _ in range(2)]
        acc = [S.sbuf([128, c.TT], F32, "acc", st) for _ in range(2)]
        ptr = [S.psum([128, 4, 128], F32, "ptr", st) for _ in range(2)]
        tmo = [S.sbuf([128, 4, 128], F32, "tmo", st) for _ in range(2)]
        ident = self.masks[:, 5, :]
        nch = c.TT // 128
        it = 0
        for cc in range(nxc):
            rw, ac = raw[cc % 2], acc[cc % 2]
            eng = "dve"
            S.dma(rw[:], self.pTc(cc), key=("raw", cc % 2), writes=[rw])
            self._conv(eng, ac, rw, mcw, mcb, cc, 5)
            S.act(ac[:], ac[:], AF.Silu, [ac], [ac])
            nxs = c.DI // 128
            nb = c.GN // 128
            if cc < nxs + nb:
                dst = self.xs_tm if cc < nxs else self.B_tm
                col0 = (cc if cc < nxs else cc - nxs) * 128
                dv = dst.rearrange("(a p) q -> p a q", p=128)
                for q0 in range(0, nch, 4):
                    nq = min(4, nch - q0)
                    p, o = ptr[it % 2], tmo[it % 2]
                    it += 1
                    for q in range(nq):
                        S.tr(p[:, q, :], ac[:, (q0 + q) * 128:(q0 + q + 1) * 128], ident, [ac, self.masks], [p])
                    S.cp("act" if it % 2 else "dve", o[:, :nq, :], p[:, :nq, :], [p], [o])
                    S.dma(dv[:, q0:q0 + nq, col0:col0 + 128], o[:, :nq, :], key=("tmo", id(o)), reads=[o])
            if nxs <= cc < nxs + nb:
                S.dma(self.B_fm[(cc - nxs) * 128:(cc - nxs + 1) * 128, :], ac[:], key=("accst", cc % 2), reads=[ac])
            elif cc >= nxs + nb:
                S.dma(self.C_fm[(cc - nxs - nb) * 128:(cc - nxs - nb + 1) * 128, :], ac[:], key=("accst", cc % 2), reads=[ac])
        for i in range(c.KD):
            for (j, b) in ((1, 0), (2, 1)):
                cc = nxc + j * c.KD + i
                S.dma(raw[b][:], self.pTc(cc), key=("raw", b), writes=[raw[b]])
                self._conv("dve", acc[b], raw[b], hcw, hcb, j * c.KD + i, 3)
            S.tt("pool", acc[0][:], acc[0][:], acc[1][:], ALU.mult, [acc[0], acc[1]], [acc[0]])
            for (nm, s0, s1) in (("C", 0, c.LC), ("L", c.LC, c.TT)):
                S.dma(self.hy[nm]["u"][i * 128:(i + 1) * 128, 0:s1 - s0], acc[0][:, s0:s1], key=("accst", 0), reads=[acc[0]])
            cc = nxc + i
            S.dma(raw[1][:], self.pTc(cc), key=("raw", 1), writes=[raw[1]])
            self._conv("dve", acc[1], raw[1], hcw, hcb, i, 3)
            for (nm, s0, s1) in (("C", 0, c.LC), ("L", c.LC, c.TT)):
                S.dma(self.hy[nm]["x0"][i * 128:(i + 1) * 128, 0:s1 - s0], acc[1][:, s0:s1], key=("accst", 1), reads=[acc[1]])
        S.flush()
        st.close()

    def ssd(self, l, d):
        c, S, din = self.cfg, self.S, self.din
        st = contextlib.ExitStack()
        H, G, R, DI, GN, HB = c.H, c.G, c.R, c.DI, c.GN, c.HB
        RW = R * 64
        U = self.masks[:, 0 if d == 0 else 2, :]
        Lm = self.masks[:, 1 if d == 0 else 3, :]
        Lmr = self.masksr[:, 1 if d == 0 else 3, :]
        ones = self.masks[:, 4, :]
        ident = self.masks[:, 5, :]
        dtb = S.sbuf([128, H], F32, "dtb", st)
        Abc = S.sbuf([128, H], F32, "Abc", st)
        S.dma(dtb[:], din["dtb"][l][:, d * H:(d + 1) * H], key="dtb", writes=[dtb])
        S.dma(Abc[:], din["alog"][l][:, d * H:(d + 1) * H], key="Abc", writes=[Abc])
        S.act(Abc[:], Abc[:], AF.Exp, [Abc], [Abc])
        S.ts("dve", Abc[:], Abc[:], -1.0, None, ALU.mult, None, [Abc], [Abc])
        St = S.sbuf([128, G, RW], F32, "St", st)
        Sr = S.sbuf([128, G, RW], F32R, "Sr", st)
        S.memset("dve", St[:], 0.0, [(St, g) for g in range(G)])
        S.cp("dve", Sr[:], St[:], [(St, g) for g in range(G)], [(Sr, g) for g in range(G)])
        xs = [S.sbuf([128, DI], F32, "xs", st)] * 2
        Btm = [S.sbuf([128, GN], F32R, "Btm", st) for _ in range(2)]
        Bfm = [S.sbuf([128, G, 128], F32R, "Bfm", st) for _ in range(2)]
        Cfm = [S.sbuf([128, G, 128], F32R, "Cfm", st) for _ in range(2)]
        dtr = [S.sbuf([128, H], F32, "dtr", st) for _ in range(2)]
        dt = S.sbuf([128, H], F32, "dt", st)
        dtA = S.sbuf([128, H], F32, "dtA", st)
        eall = S.sbuf([128, 3, H], F32, "eall", st)
        xd = S.sbuf([128, DI], F32R, "xd", st)
        xdd = S.sbuf([128, DI], F32R, "xdd", st)
        cbm = S.sbuf([128, G, 128], F32, "cbm", st)
        rhsb = [S.sbuf([128, HB, 128], F32R, "rhsb", st) for _ in range(2)]
        E = [S.sbuf([128, HB, 128], F32, "E", st) for _ in range(2)]
        MT = [S.sbuf([128, HB, 128], F32R, "MT", st) for _ in range(2)]
        ysb = S.sbuf([128, DI], F32, "ysb", st)
        ytmp = [S.sbuf([128, RW], F32, "ytmp", st) for _ in range(2)]
        ps_small = S.psum([128, 3, H], F32, "pssm", st)
        ps_cb = S.psum([128, G, 128], F32, "pscb", st)
        ps_seg = S.psum([128, HB, 128], F32, "psseg", st)
        ps_ys = [S.psum([128, RW], F32, "psy", st) for _ in range(2)]
        ps_yo = S.psum([128, RW], F32, "psyo", st)
        ps_st = S.psum([128, RW], F32, "psst", st)
        if d == 1:
            yf = S.sbuf([128, DI], F32, "yf", st)
            zs = S.sbuf([128, DI], F32, "zs", st)
            dsk = S.sbuf([128, H], F32, "dsk", st)
            mng = S.sbuf([128, DI // 128], F32, "mng", st)
            S.dma(dsk[:], din["dskip"][l], key="dsk", writes=[dsk])
            S.dma(mng[:], din["mng"][l], key="mng", writes=[mng])
            sqt = yf
            ssg = S.sbuf([128, G], F32, "ssg", st)
            ps_tr = ps_cb
            ymT = S.sbuf([128, DI // 128, 128], F32, "ymT", st)
        nch = c.TT // 128
        nctx = c.LC // 128
        if d == 0:
            order = list(range(nch))
        else:
            order = list(range(nctx - 1, -1, -1)) + list(range(nch - 1, nctx - 1, -1))
        for ci, ch in enumerate(order):
            t0 = ch * 128
            b = ci % 2
            x_, Bt, Bf, Cf, dr = xs[b], Btm[b], Bfm[b], Cfm[b], dtr[b]
            S.dma(x_[:], self.xs_tm[t0:t0 + 128, :], key=("xs", 0), writes=[x_])
            S.dma(dr[:], self.dt_tm[t0:t0 + 128, d * H:(d + 1) * H], key=("dtr", b), writes=[dr])
            S.dma(Bt[:], self.B_tm[t0:t0 + 128, :], key=("Btm", b), writes=[Bt], queue="pool")
            S.dma(Bf[:], self.B_fm[:, t0:t0 + 128].rearrange("(g n) t -> n g t", n=128), key=("Bfm", b), writes=[Bf], queue="pool")
            S.dma(Cf[:], self.C_fm[:, t0:t0 + 128].rearrange("(g n) t -> n g t", n=128), key=("Cfm", b), writes=[Cf], queue="pool")
            if d == 1:
                S.dma(yf[:], self.yf_tm[t0:t0 + 128, :], key="yf", writes=[yf])
                S.dma(zs[:], self.z_tm[t0:t0 + 128, :], key="zs", writes=[zs])
            S.tt("dve", dt[:], dr[:], dtb[:], ALU.add, [dr, dtb], [dt])
            S.act(dt[:], dt[:], AF.Exp, [dt], [dt])
            S.act(dt[:], dt[:], AF.Ln, [dt], [dt], bias=1.0, scale=1.0)
            S.tt("dve", dtA[:], dt[:], Abc[:], ALU.mult, [dt, Abc], [dtA])
            S.mm(ps_small[:, 0, :], U, dtA[:], True, True, [dtA, self.masks], [ps_small])
            S.mm(ps_small[:, 1, :], Lm, dtA[:], True, True, [dtA, self.masks], [ps_small])
            S.mm(ps_small[:, 2, :], ones, dtA[:], True, True, [dtA, self.masks], [ps_small])
            S.act(eall[:], ps_small[:], AF.Exp, [ps_small], [eall])
            S.tt("dve", xd[:].rearrange("p (h q) -> p h q", q=64), x_[:].rearrange("p (h q) -> p h q", q=64),
                 dt[:].unsqueeze(2).to_broadcast([128, H, 64]), ALU.mult, [x_, dt], [xd])
            S.tt("pool", xdd[:].rearrange("p (h q) -> p h q", q=64), xd[:].bitcast(F32).rearrange("p (h q) -> p h q", q=64),
                 eall[:, 1, :].unsqueeze(2).to_broadcast([128, H, 64]), ALU.mult, [xd, eall], [xdd])
            for g in range(G):
                S.mm(ps_cb[:, g, :], Bf[:, g, :], Cf[:, g, :], True, True, [Bf, Cf], [ps_cb])
            S.tt("dve", cbm[:], ps_cb[:], U.unsqueeze(1).to_broadcast([128, G, 128]), ALU.mult, [ps_cb, self.masks], [cbm])
            it = 0
            for g in range(G):
                ps_y = ps_ys[g % 2]
                for hb in range(R // HB):
                    h0 = g * R + hb * HB
                    rb, Eb, Mb = rhsb[it % 2], E[it % 2], MT[it % 2]
                    it += 1
                    S.tt("pool", rb[:], U.unsqueeze(1).to_broadcast([128, HB, 128]),
                         dtA[:, h0:h0 + HB].unsqueeze(2).to_broadcast([128, HB, 128]), ALU.mult, [dtA, self.masks], [rb])
                    for j in range(HB):
                        S.mm(ps_seg[:, j, :], Lmr, rb[:, j, :], True, True, [rb, self.masksr], [ps_seg])
                    S.act(Eb[:], ps_seg[:], AF.Exp, [ps_seg], [Eb])
                    S.tt("dve", Mb[:], Eb[:], cbm[:, g, :].unsqueeze(1).to_broadcast([128, HB, 128]), ALU.mult, [Eb, cbm], [Mb])
                    for j in range(HB):
                        hh = h0 + j
                        S.mm(ps_y[:, (hb * HB + j) * 64:(hb * HB + j + 1) * 64], Mb[:, j, :], xd[:, hh * 64:(hh + 1) * 64],
                             True, True, [Mb, xd], [ps_y])
                S.mm(ps_yo[:], Cf[:, g, :], Sr[:, g, :], True, True, [Cf, (Sr, g)], [ps_yo])
                yt = ytmp[g % 2]
                S.tt("dve", yt[:].rearrange("p (h q) -> p h q", q=64), ps_yo[:].rearrange("p (h q) -> p h q", q=64),
                     eall[:, 0, g * R:(g + 1) * R].unsqueeze(2).to_broadcast([128, R, 64]), ALU.mult, [ps_yo, eall], [yt])
                S.tt("dve", ysb[:, g * RW:(g + 1) * RW], yt[:], ps_y[:], ALU.add, [yt, ps_y], [(ysb, g)])
                S.mm(ps_st[:], Bt[:, g * 128:(g + 1) * 128], xdd[:, g * RW:(g + 1) * RW], True, True, [Bt, xdd], [ps_st])
                S.tt("pool", St[:, g, :].rearrange("p (h q) -> p h q", q=64), St[:, g, :].rearrange("p (h q) -> p h q", q=64),
                     eall[:, 2, g * R:(g + 1) * R].unsqueeze(2).to_broadcast([128, R, 64]), ALU.mult, [(St, g), eall], [(St, g)])
                S.tt("dve", St[:, g, :], St[:, g, :], ps_st[:], ALU.add, [(St, g), ps_st], [(St, g)])
                S.cp("act", Sr[:, g, :], St[:, g, :], [(St, g)], [(Sr, g)])
            yall = [(ysb, g) for g in range(G)]
            if d == 0:
                S.dma(self.yf_tm[t0:t0 + 128, :], ysb[:], key="ysb", reads=yall)
            else:
                S.tt("dve", ysb[:], ysb[:], yf[:], ALU.add, yall + [yf], yall)
                S.tt("pool", yf[:].rearrange("p (h q) -> p h q", q=64), x_[:].rearrange("p (h q) -> p h q", q=64),
                     dsk[:].unsqueeze(2).to_broadcast([128, H, 64]), ALU.mult, [x_, dsk, yf], [yf])
                S.tt("dve", ysb[:], ysb[:], yf[:], ALU.add, yall + [yf], yall)
                S.tt("pool", ysb[:], ysb[:], zs[:], ALU.mult, yall + [zs], yall)
                S.act(sqt[:], ysb[:], AF.Square, yall + [sqt], [sqt])
                S.op("dve", lambda e: e.tensor_reduce(out=ssg[:], in_=sqt[:].rearrange("p (g q) -> p g q", g=G), axis=AX.X, op=ALU.add), [sqt], [ssg])
                S.ts("dve", ssg[:], ssg[:], 1.0 / (DI // G), c.EPS, ALU.mult, ALU.add, [ssg], [ssg])
                S.act(ssg[:], ssg[:], AF.Sqrt, [ssg], [ssg])
                S.op("dve", lambda e: e.reciprocal(out=ssg[:], in_=ssg[:]), [ssg], [ssg])
                S.tt("dve", ysb[:].rearrange("p (g q) -> p g q", g=G), ysb[:].rearrange("p (g q) -> p g q", g=G),
                     ssg[:].unsqueeze(2).to_broadcast([128, G, DI // G]), ALU.mult, yall + [ssg], yall)
                for q0 in range(0, DI // 128, 4):
                    for q in range(4):
                        S.tr(ps_tr[:, q, :], ysb[:, (q0 + q) * 128:(q0 + q + 1) * 128], ident, yall + [self.masks], [ps_tr])
                    for q in range(4):
                        S.act(ymT[:, q0 + q, :], ps_tr[:, q, :], AF.Identity, [ps_tr, mng], [ymT], scale=mng[:, q0 + q:q0 + q + 1])
                S.dma(self.ym_fm[:, t0:t0 + 128].rearrange("(a p) t -> p a t", p=128), ymT[:], key="ymT", reads=[ymT])
        S.flush()
        st.close()

    def _sin(self, a_, out, ki, kf, w=None):
        S = self.S
        S.ts("dve", a_[:], a_[:], 1.0 / (2.0 * math.pi), 64.5, ALU.mult, ALU.add, [a_], [a_])
        S.cp("dve", ki[:], a_[:], [a_], [ki])
        S.cp("dve", kf[:], ki[:], [ki], [kf])
        S.tt("dve", a_[:], a_[:], kf[:], ALU.subtract, [a_, kf], [a_])
        S.ts("dve", kf[:], a_[:], 0.0, None, ALU.is_lt, None, [a_], [kf])
        S.tt("dve", a_[:], a_[:], kf[:], ALU.add, [a_, kf], [a_])
        S.ts("dve", a_[:], a_[:], 0.0, 0.999999, ALU.max, ALU.min, [a_], [a_])
        S.act(out, a_[:], AF.Sin, [a_, self.cpi], w or [a_], bias=self.cpi[0:64, 0:1], scale=2.0 * math.pi)

    def filter_gen(self, l, nm):
        c, S, din = self.cfg, self.S, self.din
        hh = self.hy[nm]
        Lx = hh["L"]
        N2 = 2 * Lx
        bs = min(512, Lx)
        nblk = N2 // bs
        st = contextlib.ExitStack()
        w1 = S.sbuf([33, 64], F32, "w1", st)
        w2 = S.sbuf([64, 64], F32, "w2", st)
        w3 = S.sbuf([64, 2 * c.D], F32, "w3", st)
        b1 = S.sbuf([64, 1], F32, "b1", st)
        b2 = S.sbuf([64, 1], F32, "b2", st)
        fr = S.sbuf([64, 1], F32, "fr", st)
        nd = S.sbuf([128, c.KD], F32, "nd", st)
        for (t, k) in ((w1, "hw1"), (w2, "hw2"), (w3, "hw3"), (b1, "hb1"), (b2, "hb2"), (fr, "hfr")):
            S.dma(t[:], din[k][l], key=k, writes=[t])
        S.dma(nd[:], din["ndelta"], key="nd", writes=[nd])
        hid = S.sbuf([64, N2], F32, "hid", st)
        ft = [S.sbuf([33, bs], F32, "ft", st) for _ in range(2)]
        a1 = [S.sbuf([64, bs], F32, "a1", st) for _ in range(2)]
        ph = [S.psum([64, bs], F32, "ph", st) for _ in range(2)]
        ki = S.sbuf([64, bs], I32, "ki", st)
        kf = S.sbuf([64, bs], F32, "kf", st)
        for nb in range(nblk):
            f_, a_, p_ = ft[nb % 2], a1[nb % 2], ph[nb % 2]
            S.dma(f_[:], din["feats" + nm][:, nb * bs:(nb + 1) * bs], key=("ft", nb % 2), writes=[f_])
            S.mm(p_[:], w1[:], f_[:], True, True, [w1, f_], [p_])
            S.ts("dve", a_[:], p_[:], b1[:, 0:1], fr[:, 0:1], ALU.add, ALU.mult, [p_, b1, fr], [a_])
            self._sin(a_, a_[:], ki, kf)
            S.mm(p_[:], w2[:], a_[:], True, True, [w2, a_], [p_])
            S.ts("dve", a_[:], p_[:], b2[:, 0:1], fr[:, 0:1], ALU.add, ALU.mult, [p_, b2, fr], [a_])
            self._sin(a_, hid[:, nb * bs:(nb + 1) * bs], ki, kf, [hid])
        filt = [S.sbuf([128, N2], F32, "filt", st)]
        tp = [S.sbuf([128, bs], F32, "tp", st) for _ in range(2)]
        win = [S.sbuf([128, bs], F32, "win", st) for _ in range(2)]
        pf = [S.psum([128, bs], F32, "pf", st) for _ in range(2)]
        ssb = S.sbuf([128, nblk], F32, "ssb", st)
        ss = S.sbuf([128, 1], F32, "ss", st)
        for i in range(c.KD):
            fl = filt[0]
            S.memset("pool", ssb[:], 0.0, [ssb])
            for nb in range(nblk):
                p_, w_, t_ = pf[nb % 2], win[nb % 2], tp[nb % 2]
                half = 0 if (nb * bs) < Lx else 1
                S.mm(p_[:], w3[:, half * c.D + i * 128:half * c.D + (i + 1) * 128], hid[:, nb * bs:(nb + 1) * bs], True, True, [w3, hid], [p_])
                S.dma(t_[:], din["tpos" + nm][:, nb * bs:(nb + 1) * bs], key=("tp", nb % 2), writes=[t_])
                S.act(w_[:], t_[:], AF.Exp, [t_, nd], [w_], scale=nd[:, i:i + 1])
                S.tt("dve", fl[:, nb * bs:(nb + 1) * bs], p_[:], w_[:], ALU.mult, [p_, w_], [fl])
                S.act(w_[:], fl[:, nb * bs:(nb + 1) * bs], AF.Square, [fl], [w_, ssb], accum=ssb[:, nb:nb + 1])
            S.op("dve", lambda e: e.tensor_reduce(out=ss[:], in_=ssb[:], axis=AX.X, op=ALU.add), [ssb], [ss])
            S.ts("dve", ss[:], ss[:], c.EPS, None, ALU.add, None, [ss], [ss])
            S.act(ss[:], ss[:], AF.Sqrt, [ss], [ss])
            S.op("dve", lambda e: e.reciprocal(out=ss[:], in_=ss[:]), [ss], [ss])
            S.ts("dve", fl[:], fl[:], ss[:, 0:1], None, ALU.mult, None, [fl, ss], [fl])
            S.dma(hh["filt"][i * 128:(i + 1) * 128, :], fl[:], key=("filt", 0), reads=[fl])
        S.flush()
        st.close()

    def _cmul(self, X, Tre, Tim, P_, Q_, out, rT):
        S = self.S
        S.tt("dve", P_[:], X[:], Tre, ALU.mult, [X] + rT, [P_])
        S.tt("dve", Q_[:], X[:], Tim, ALU.mult, [X] + rT, [Q_])
        S.tt("pool", out[:, :, 0, :], P_[:, :, 0, :], Q_[:, :, 1, :], ALU.subtract, [P_, Q_], [(out, 0)])
        S.tt("pool", out[:, :, 1, :], Q_[:, :, 0, :], P_[:, :, 1, :], ALU.add, [P_, Q_], [(out, 1)])

    def fft_pass(self, l, nm, is_filter):
        c, S, din = self.cfg, self.S, self.din
        hh = self.hy[nm]
        Lx, N1, P = hh["L"], hh["N1"], hh["P"]
        npass = c.D // P
        PG = 2
        assert npass % PG == 0 and hh["HP"] % PG == 0
        st = contextlib.ExitStack()
        tabs = {}
        for k, shp in (("F1bd", [128, 2, 128]), ("F2", [128, 3, 128]), ("Finv2", [128, 4, 128]), ("Finv1bd", [128, 2, 128])):
            tabs[k] = S.sbuf(shp, F32R, k, st)
            S.dma(tabs[k][:], din[k + nm], key=k, writes=[tabs[k]], queue="pool")
        for k in ("Tw1", "Tw2"):
            tabs[k] = S.sbuf([128, 2, 128], F32, k, st)
            S.dma(tabs[k][:], din[k + nm], key=k, writes=[tabs[k]])
        hbp = S.sbuf([128, npass], F32, "hbp", st)
        S.dma(hbp[:], din["hbias" + nm][l], key="hbp", writes=[hbp])
        NB = 2
        shp4 = [128, PG, 2, 128]
        u = [S.sbuf([128, PG, 128], F32R, "u", st) for _ in range(NB)]
        x0 = [S.sbuf([128, PG, 128], F32, "x0", st) for _ in range(NB)]
        Ht = [S.sbuf(shp4, F32, "Ht", st) for _ in range(NB)]
        Pt = [S.sbuf(shp4, F32, "Pt", st) for _ in range(NB)]
        Qt = [S.sbuf(shp4, F32, "Qt", st) for _ in range(NB)]
        Ap = [S.sbuf(shp4, F32R, "Ap", st) for _ in range(NB)]
        Yt = [S.sbuf(shp4, F32R, "Yt", st) for _ in range(NB)]
        Cp = [S.sbuf(shp4, F32R, "Cp", st) for _ in range(NB)]
        yo = [S.sbuf([128, PG, 128], F32, "yo", st) for _ in range(NB)]
        psA_ = [S.psum(shp4, F32, "psA", st) for _ in range(NB)]
        psX_ = [S.psum(shp4, F32, "psX", st) for _ in range(NB)]
        psC_ = [S.psum(shp4, F32, "psC", st) for _ in range(NB)]
        psY_ = [S.psum([128, 4, 128], F32, "psY", st) for _ in range(NB)]
        src = hh["filt"] if is_filter else hh["u"]
        F1, F2, Fi2, Fi1 = tabs["F1bd"], tabs["F2"], tabs["Finv2"], tabs["Finv1bd"]
        bc = lambda t, i: t[:, i, :].unsqueeze(1).unsqueeze(1).to_broadcast(shp4)
        def vw(t, gi):
            c0 = gi * PG * P
            return t[c0:c0 + PG * P, :].rearrange("(g c) (a b) -> (c a) g b", g=PG, b=128)

        def hslice(gi):
            ps0 = gi * PG
            return hh["H"][ps0 // hh["HP"]][ps0 % hh["HP"]:ps0 % hh["HP"] + PG].rearrange("g p (r k) -> p g r k", r=2)

        def loads(gi):
            b = gi % NB
            S.dma(u[b][:], vw(src, gi), key=("u", b), writes=[u[b]], queue="pool")
            if not is_filter:
                S.dma(Ht[b][:], hslice(gi), key=("Ht", b), writes=[Ht[b]])
                S.dma(x0[b][:], vw(hh["x0"], gi), key=("x0", b), writes=[x0[b]])

        ngrp = npass // PG
        loads(0)
        for gi in range(ngrp):
            b = gi % NB
            ps0 = gi * PG
            psA, psX, psC = psA_[b], psX_[b], psC_[b]
            psY = psY_[b][:, 0:PG, :]
            psYr = psY_[b]
            view = lambda t: vw(t, gi)
            if gi + 1 < ngrp and not is_filter:
                loads(gi + 1)
            for g in range(PG):
                S.mm(psA[:, g, :, :], u[b][:, g, :], F1[:], True, True, [u[b], F1], [psA])
            self._cmul(psA, bc(tabs["Tw1"], 0), bc(tabs["Tw1"], 1), Pt[b], Qt[b], Ap[b], [tabs["Tw1"]])
            apr = [(Ap[b], 0), (Ap[b], 1)]
            for g in range(PG):
                S.mm(psX[:, g, :, :], F2[:, 0, :], Ap[b][:, g, :, :], True, False, apr + [F2], [psX])
                S.mm(psX[:, g, 0, :], F2[:, 2, :], Ap[b][:, g, 1, :], False, False, apr + [F2], [psX])
                S.mm(psX[:, g, 1, :], F2[:, 1, :], Ap[b][:, g, 0, :], False, True, apr + [F2], [psX])
            hsl = hslice(gi)
            if is_filter:
                S.cp("act", Ht[b][:], psX[:], [psX], [Ht[b]])
                S.dma(hsl, Ht[b][:], key=("Ht", b), reads=[Ht[b]])
                if gi + 1 < ngrp:
                    loads(gi + 1)
                continue
            hre = Ht[b][:, :, 0, :].unsqueeze(2).to_broadcast(shp4)
            him = Ht[b][:, :, 1, :].unsqueeze(2).to_broadcast(shp4)
            self._cmul(psX, hre, him, Pt[b], Qt[b], Yt[b], [Ht[b]])
            ytr = [(Yt[b], 0), (Yt[b], 1)]
            for g in range(PG):
                S.mm(psC[:, g, :, :], Yt[b][:, g, 0, :], Fi2[:, 0:2, :], True, False, ytr + [Fi2], [psC])
                S.mm(psC[:, g, :, :], Yt[b][:, g, 1, :], Fi2[:, 2:4, :], False, True, ytr + [Fi2], [psC])
            self._cmul(psC, bc(tabs["Tw2"], 0), bc(tabs["Tw2"], 1), Pt[b], Qt[b], Cp[b], [tabs["Tw2"]])
            cpr = [(Cp[b], 0), (Cp[b], 1)]
            for g in range(PG):
                S.mm(psY[:, g, :], Fi1[:, 0, :], Cp[b][:, g, 0, :], True, False, cpr + [Fi1], [psYr])
                S.mm(psY[:, g, :], Fi1[:, 1, :], Cp[b][:, g, 1, :], False, True, cpr + [Fi1], [psYr])
            S.tt("pool", yo[b][:], u[b][:].bitcast(F32), hbp[:, ps0:ps0 + PG].unsqueeze(2).to_broadcast([128, PG, 128]), ALU.mult,
                 [u[b], hbp], [yo[b]])
            S.tt("dve", yo[b][:], yo[b][:], psY, ALU.add, [yo[b], psYr], [yo[b]])
            S.tt("pool", yo[b][:], yo[b][:], x0[b][:], ALU.mult, [yo[b], x0[b]], [yo[b]])
            S.dma(view(hh["yh"]), yo[b][:], key=("yo", b), reads=[yo[b]])
        S.flush()
        st.close()

    def out_proj(self, l):
        c, S, din = self.cfg, self.S, self.din
        st = contextlib.ExitStack()
        T = c.TF
        KM = c.DI // 128
        ym = S.sbuf([128, KM, T], F32R, "ym", st)
        yh = S.sbuf([128, c.KD, T], F32R, "yh", st)
        mg = S.sbuf([128, c.KD, T], F32R, "mg", st)
        wm = [S.sbuf([128, KM, 128], F32R, "wm", st) for _ in range(2)]
        wh = [S.sbuf([128, c.KD, 128], F32R, "wh", st) for _ in range(2)]
        ga = [S.sbuf([128, T], F32, "ga", st) for _ in range(2)]
        gb = [S.sbuf([128, T], F32, "gb", st) for _ in range(2)]
        t1 = [S.sbuf([128, T], F32, "t1", st) for _ in range(2)]
        t2 = [S.sbuf([128, T], F32, "t2", st) for _ in range(2)]
        xo = [S.sbuf([128, T], F32, "xo", st) for _ in range(2)]
        pP = [S.psum([128, T], F32, "pP", st) for _ in range(2)]
        pQ = [S.psum([128, T], F32, "pQ", st) for _ in range(2)]
        pZ = [S.psum([128, T], F32, "pZ", st) for _ in range(2)]
        g0 = (c.XBC + 3 * c.D)
        md = self.mods[l]
        for (t0, Tb, r) in c.blocks(T):
            S.dma(ym[:, :, :Tb], self.ym_fm[:, t0:t0 + Tb].rearrange("(k p) t -> p k t", p=128), key="ym", writes=[ym], queue="pool")
            hh = self.hy["C" if r == 1 else "L"]
            ts0 = t0 if r == 1 else t0 - c.LC
            S.dma(yh[:, :, :Tb], hh["yh"][:, ts0:ts0 + Tb].rearrange("(k p) t -> p k t", p=128), key="yh", writes=[yh], queue="pool")
            for cc in range(c.KD):
                b = cc % 2
                S.dma(wm[b][:], din["mwo"][l, cc], key=("wm", b), writes=[wm[b]], queue="pool")
                S.dma(wh[b][:], din["hwo"][l, cc], key=("wh", b), writes=[wh[b]], queue="pool")
                S.dma(ga[b][:, :Tb], self.pTc(g0 // 128 + cc)[:, t0:t0 + Tb], key=("ga", b), writes=[ga[b]])
                S.dma(gb[b][:, :Tb], self.pTc((g0 + c.D) // 128 + cc)[:, t0:t0 + Tb], key=("gb", b), writes=[gb[b]])
                for k in range(KM):
                    S.mm(pP[b][:, :Tb], wm[b][:, k, :], ym[:, k, :Tb], k == 0, k == KM - 1, [wm[b], ym], [pP[b]])
                for k in range(c.KD):
                    S.mm(pQ[b][:, :Tb], wh[b][:, k, :], yh[:, k, :Tb], k == 0, k == c.KD - 1, [wh[b], yh], [pQ[b]])
                S.tt("dve", t1[b][:, :Tb], pP[b][:, :Tb], ga[b][:, :Tb], ALU.mult, [pP[b], ga[b]], [t1[b]])
                S.tt("dve", t2[b][:, :Tb], pQ[b][:, :Tb], gb[b][:, :Tb], ALU.mult, [pQ[b], gb[b]], [t2[b]])
                S.tt("dve", mg[:, cc, :Tb], t1[b][:, :Tb], t2[b][:, :Tb], ALU.add, [t1[b], t2[b]], [(mg, cc)])
            mga = [(mg, cc) for cc in range(c.KD)]
            for cc in range(c.KD):
                b = cc % 2
                S.dma(wh[b][:], din["wmo"][l, cc], key=("wh", b), writes=[wh[b]], queue="pool")
                S.dma(ga[b][:, :Tb], self.xT[cc * 128:(cc + 1) * 128, t0:t0 + Tb], key=("ga", b), writes=[ga[b]])
                for k in range(c.KD):
                    S.mm(pZ[b][:, :Tb], wh[b][:, k, :], mg[:, k, :Tb], k == 0, k == c.KD - 1, [wh[b]] + mga, [pZ[b]])
                S.stt("dve", xo[b][:, :Tb], pZ[b][:, :Tb], md[:, 2 * c.KD + cc, r:r + 1], ga[b][:, :Tb], ALU.mult, ALU.add, [pZ[b], md, ga[b]], [xo[b]])
                S.dma(self.xT[cc * 128:(cc + 1) * 128, t0:t0 + Tb], xo[b][:, :Tb], key=("xo", b), reads=[xo[b]])
        S.flush()
        st.close()

    def ffn(self, l):
        c, S, din = self.cfg, self.S, self.din
        st = contextlib.ExitStack()
        T = c.TF
        KF = c.FF // 128
        nbk = self.norm_block(st, T, "f")
        h = nbk["h"]
        a = S.sbuf([128, KF, T], BF16, "a", st)
        wg = [S.sbuf([128, c.KD, 128], F32R, "wg", st) for _ in range(2)]
        wu = [S.sbuf([128, c.KD, 128], F32R, "wu", st) for _ in range(2)]
        wd = [S.sbuf([128, KF, 128], BF16, "wd", st) for _ in range(2)]
        sg = [S.sbuf([128, T], F32, "sg", st) for _ in range(2)]
        xi = [S.sbuf([128, T], F32, "xi", st) for _ in range(2)]
        xo = [S.sbuf([128, T], F32, "xo", st) for _ in range(2)]
        pG = [S.psum([128, T], F32, "pG", st) for _ in range(2)]
        pU = [S.psum([128, T], F32, "pU", st) for _ in range(2)]
        pD = [S.psum([128, T], F32, "pD", st) for _ in range(2)]
        md = self.mods[l]
        for (t0, Tb, r) in c.blocks(T):
            self.norm_run(nbk, t0, Tb, r, self.A2[l], md, 3 * c.KD)
            for j in range(KF):
                b = j % 2
                S.dma(wg[b][:], din["wg"][l, j], key=("wg", b), writes=[wg[b]], queue="pool")
                S.dma(wu[b][:], din["wu"][l, j], key=("wu", b), writes=[wu[b]], queue="pool")
                for k in range(c.KD):
                    S.mm(pG[b][:, :Tb], wg[b][:, k, :], h[:, k, :Tb], k == 0, k == c.KD - 1, [wg[b], h], [pG[b]])
                for k in range(c.KD):
                    S.mm(pU[b][:, :Tb], wu[b][:, k, :], h[:, k, :Tb], k == 0, k == c.KD - 1, [wu[b], h], [pU[b]])
                S.act(sg[b][:, :Tb], pG[b][:, :Tb], AF.Silu, [pG[b]], [sg[b]])
                S.tt("dve", a[:, j, :Tb], sg[b][:, :Tb], pU[b][:, :Tb], ALU.mult, [sg[b], pU[b]], [(a, j)])
            aa = [(a, j) for j in range(KF)]
            for cc in range(c.KD):
                b = cc % 2
                S.dma(wd[b][:], din["wd"][l, cc], key=("wd", b), writes=[wd[b]], queue="pool")
                S.dma(xi[b][:, :Tb], self.xT[cc * 128:(cc + 1) * 128, t0:t0 + Tb], key=("xi", b), writes=[xi[b]])
                for k in range(KF):
                    S.mm(pD[b][:, :Tb], wd[b][:, k, :], a[:, k, :Tb], k == 0, k == KF - 1, [wd[b]] + aa, [pD[b]])
                S.stt("dve", xo[b][:, :Tb], pD[b][:, :Tb], md[:, 5 * c.KD + cc, r:r + 1], xi[b][:, :Tb], ALU.mult, ALU.add, [pD[b], md, xi[b]], [xo[b]])
                S.dma(self.xT[cc * 128:(cc + 1) * 128, t0:t0 + Tb], xo[b][:, :Tb], key=("xo", b), reads=[xo[b]])
        S.flush()
        st.close()

    def final(self):
        c, S, din = self.cfg, self.S, self.din
        st = contextlib.ExitStack()
        T = c.TF
        nbk = self.norm_block(st, T, "z", F32)
        gf = S.sbuf([128, c.KD, 1], F32, "gf", st)
        S.dma(gf[:, :, 0], din["gfin"], key="gf", writes=[gf])
        for (t0, Tb, r) in c.blocks(T):
            if r == 1:
                continue
            self.norm_run(nbk, t0, Tb, 0, gf, None, 0)
            S.dma(self.out[:, t0 - c.LC:t0 - c.LC + Tb].rearrange("(k p) t -> p k t", p=128), nbk["h"][:, :, :Tb],
                  key="outst", reads=[nbk["h"]])
        S.flush()
        st.close()


_CACHE = {}


def run_cfg(cfg, inputs, n_cores=None):
    B = cfg.B
    maps = [host_prepare(cfg, inputs, 0)]
    for b in range(1, B):
        mb = dict(maps[0])
        xT = np.concatenate([inputs["ctx"][b].T, inputs["x"][b].T], axis=1)
        mb["xin"] = np.ascontiguousarray(xT).astype(np.float32)
        cc = np.stack([inputs["c"][b], inputs["c_ctx"]], axis=1)
        mb["cT"] = np.ascontiguousarray(cc.reshape(cfg.KD, 128, 2).transpose(1, 0, 2)).astype(np.float32)
        maps.append(mb)
    shapes = {k: v.shape for k, v in maps[0].items()}
    key = (cfg.D, cfg.L, cfg.LC, cfg.DEPTH)
    if key not in _CACHE:
        import os
        ns = os.environ.get("K_STAGES")
        _CACHE[key] = Prog(cfg, shapes).build(int(ns) if ns else None)
    nc = _CACHE[key]
    res = run_bass_kernel_spmd(nc, maps, core_ids=list(range(B)))
    out = np.stack([np.ascontiguousarray(res.results[b]["out"].T) for b in range(B)])
    return out.astype(np.float32)


def kernel(**inputs):
    inputs = {k: np.asarray(v) for k, v in inputs.items()}
    cfg = Cfg()
    return run_cfg(cfg, inputs)
```

```python
import contextlib
import math
import numpy as np
import concourse.bass as bass
import concourse.mybir as mybir
from concourse.bass_utils import run_bass_kernel_spmd

F32 = mybir.dt.float32
F32R = mybir.dt.float32r
BF16 = mybir.dt.bfloat16
I32 = mybir.dt.int32
ALU = mybir.AluOpType
AF = mybir.ActivationFunctionType
AX = mybir.AxisListType

COMPUTE = ("pe", "act", "dve", "pool")
QUEUES = ("sp",) + COMPUTE


class _Op:
    __slots__ = ("eng", "fn", "deps", "signal", "dma_key", "dma_val", "sigval")

    def __init__(self, eng, fn):
        self.eng = eng
        self.fn = fn
        self.deps = []
        self.signal = False
        self.dma_key = None
        self.dma_val = 0
        self.sigval = 0


class Sched:
    def __init__(self, nc):
        self.nc = nc
        self.stack = contextlib.ExitStack()
        self.ops = {e: [] for e in QUEUES}
        self.last_w = {}
        self.readers = {}
        self.sems = {}
        self.sig_count = {e: 0 for e in COMPUTE}
        self.dma_count = {}
        self.dma_last = {}
        self.waited = {e: {} for e in QUEUES}
        self.n_tiles = 0
        self.n_emitted = 0
        self.slot_of = {}

    def sem(self, key):
        if key not in self.sems:
            self.sems[key] = self.stack.enter_context(self.nc.semaphore("s%d" % len(self.sems)))
        return self.sems[key]

    def sbuf(self, shape, dtype, name="t", stack=None):
        self.n_tiles += 1
        return (stack or self.stack).enter_context(
            self.nc.sbuf_tensor("%s_%d" % (name, self.n_tiles), list(shape), dtype))

    def psum(self, shape, dtype, name="p", stack=None):
        self.n_tiles += 1
        return (stack or self.stack).enter_context(
            self.nc.psum_tensor("%s_%d" % (name, self.n_tiles), list(shape), dtype))

    @staticmethod
    def _k(r):
        if isinstance(r, tuple):
            return tuple(Sched._k(x) for x in r)
        if isinstance(r, (str, int)):
            return r
        return ("id", id(r))

    def _add(self, eng, fn, reads, writes):
        op = _Op(eng, fn)
        deps = []
        for r in reads:
            w = self.last_w.get(r)
            if w is not None:
                deps.append(w)
        for r in writes:
            w = self.last_w.get(r)
            if w is not None:
                deps.append(w)
            deps.extend(self.readers.get(r, ()))
        seen = set()
        for d in deps:
            if d is op or id(d) in seen:
                continue
            seen.add(id(d))
            if d.eng == eng and d.dma_key is None and eng in ("pe", "sp"):
                continue
            op.deps.append(d)
        for r in writes:
            self.last_w[r] = op
            self.readers[r] = []
        for r in reads:
            if r not in writes:
                self.readers.setdefault(r, []).append(op)
        self.ops[eng].append(op)
        return op

    def op(self, eng, fn, reads=(), writes=()):
        return self._add(eng, fn, tuple(self._k(r) for r in reads), tuple(self._k(r) for r in writes))

    def dma(self, out, in_, key, reads=(), writes=(), queue="sp"):
        def fn(e, out=out, in_=in_):
            return e.dma_start(out=out, in_=in_)
        op = self._add(queue, fn, tuple(self._k(r) for r in reads), tuple(self._k(r) for r in writes))
        key = self._k(key)
        if key not in self.slot_of:
            self.slot_of[key] = len(self.slot_of)
        key = self.slot_of[key]
        prev = self.dma_last.get(key)
        if prev is not None and all(d is not prev for d in op.deps):
            op.deps.append(prev)
        op.dma_key = key
        self.dma_count[key] = self.dma_count.get(key, 0) + 16
        op.dma_val = self.dma_count[key]
        self.dma_last[key] = op
        self.sem(("dma", key))
        return op

    def flush(self):
        if self.dma_last:
            fin = self._add("sp", None, (), ())
            for d in self.dma_last.values():
                if all(x is not d for x in fin.deps):
                    fin.deps.append(d)
        for e in QUEUES:
            for op in self.ops[e]:
                for d in op.deps:
                    if d.dma_key is None:
                        d.signal = True
        for e in COMPUTE:
            for op in self.ops[e]:
                if op.dma_key is None and op.signal:
                    self.sig_count[e] += 1
                    op.sigval = self.sig_count[e]
            self.sem(("eng", e))
        handles = {"sp": "sync", "pe": "tensor", "act": "scalar", "dve": "vector", "pool": "gpsimd"}
        with self.nc.Block() as block:
            for e in QUEUES:
                todo = self.ops[e]
                if not todo:
                    continue

                def body(eng, e=e, todo=todo):
                    waited = self.waited[e]
                    for op in todo:
                        for d in op.deps:
                            if d.dma_key is not None:
                                k, v = ("dma", d.dma_key), d.dma_val
                            else:
                                k, v = ("eng", d.eng), d.sigval
                            if waited.get(k, 0) >= v:
                                continue
                            waited[k] = v
                            eng.wait_ge(self.sems[k], v)
                        if op.fn is None:
                            continue
                        inst = op.fn(eng)
                        if op.dma_key is not None:
                            inst.then_inc(self.sems[("dma", op.dma_key)], 16)
                        elif op.signal:
                            inst.then_inc(self.sems[("eng", e)], 1)

                getattr(block, handles[e])(body)
                self.n_emitted += len(todo)
        self.ops = {e: [] for e in QUEUES}
        self.last_w.clear()
        self.readers.clear()
        self.dma_last.clear()
        self.slot_of = {}

    def tt(self, eng, out, a, b, op, r, w):
        self.op(eng, lambda e: e.tensor_tensor(out=out, in0=a, in1=b, op=op), r, w)

    def ts(self, eng, out, a, s1, s2, op0, op1, r, w):
        if s2 is None:
            self.op(eng, lambda e: e.tensor_scalar(out=out, in0=a, scalar1=s1, scalar2=None, op0=op0), r, w)
        else:
            self.op(eng, lambda e: e.tensor_scalar(out=out, in0=a, scalar1=s1, scalar2=s2, op0=op0, op1=op1), r, w)

    def stt(self, eng, out, a, sc, b, op0, op1, r, w):
        self.op(eng, lambda e: e.scalar_tensor_tensor(out=out, in0=a, scalar=sc, in1=b, op0=op0, op1=op1), r, w)

    def act(self, out, in_, func, r, w, bias=None, scale=None, accum=None):
        kw = {}
        if bias is not None:
            kw["bias"] = bias
        if scale is not None:
            kw["scale"] = scale
        if accum is not None:
            kw["accum_out"] = accum
        self.op("act", lambda e: e.activation(out=out, in_=in_, func=func, **kw), r, w)

    def cp(self, eng, out, in_, r, w):
        if eng == "act":
            self.op("act", lambda e: e.copy(out=out, in_=in_), r, w)
        else:
            self.op(eng, lambda e: e.tensor_copy(out=out, in_=in_), r, w)

    def mm(self, out, lhsT, rhs, start, stop, r, w):
        self.op("pe", lambda e: e.matmul(out, lhsT=lhsT, rhs=rhs, start=start, stop=stop), r, w)

    def tr(self, out, in_, ident, r, w):
        self.op("pe", lambda e: e.transpose(out, in_, ident), r, w)

    def memset(self, eng, ap, val, w):
        self.op(eng, lambda e: e.memset(ap, val), (), w)


class Cfg:
    def __init__(s, D=2048, L=8192, LC=256, DEPTH=4, B=4):
        s.D, s.L, s.LC, s.DEPTH, s.B = D, L, LC, DEPTH, B
        s.KD = D // 128
        s.DI = 2 * D
        s.H = s.DI // 64
        s.G = 8
        s.R = s.H // s.G
        s.N = 128
        s.GN = s.G * s.N
        s.XBC = s.DI + 2 * s.GN
        s.FF = -(-8 * D // (3 * 256)) * 256
        s.OFF_DT = s.XBC
        s.OFF_Z = s.OFF_DT + 2 * s.H
        s.OFF_HY = s.OFF_Z + s.DI
        s.OFF_GATE = s.OFF_HY + 3 * D
        s.IN_COLS = s.OFF_GATE + 2 * D
        s.TT = LC + L
        s.NFM = s.XBC + 3 * D + 2 * D
        s.NTM = s.DI + 2 * s.H
        s.EPS = 1e-6
        s.TB = 1024 if L >= 1024 else L
        s.TF = 512
        s.HB = min(4, s.R)

    def blocks(s, T):
        out = [(0, s.LC, 1)]
        t = s.LC
        while t < s.TT:
            out.append((t, min(T, s.TT - t), 0))
            t += T
        return out


def _tile_w(W, cw):
    K, C = W.shape
    assert K % 128 == 0 and C % cw == 0
    return np.ascontiguousarray(W.reshape(K // 128, 128, C // cw, cw).transpose(2, 1, 0, 3)).astype(np.float32)


def _cols(v, n=128):
    return np.ascontiguousarray(v.reshape(-1, n).T).astype(np.float32)


def _rep(v):
    return np.ascontiguousarray(np.broadcast_to(np.asarray(v, np.float32).reshape(1, -1), (128, v.size)))


def _pos_embed(cfg, grid_w=64, base=10000.0):
    rows = cfg.L // grid_w
    r, col = np.meshgrid(np.arange(rows), np.arange(grid_w), indexing="ij")
    quarter = cfg.D // 4
    omega = (1.0 / (np.float32(base) ** (np.arange(quarter, dtype=np.float32) / np.float32(quarter)))).astype(np.float32)

    def ax(pos):
        ang = pos.reshape(-1)[:, None].astype(np.float32) * omega[None, :]
        return np.concatenate([np.sin(ang), np.cos(ang)], axis=-1)

    return np.concatenate([ax(r), ax(col)], axis=-1).astype(np.float32)


def _fft_tables(N1):
    N = N1 * 128
    P = 128 // N1
    i1 = np.arange(N1)
    i2 = np.arange(128)
    a1 = 2 * np.pi * np.outer(i1, i1) / N1
    eye = np.eye(P)
    F1bd = np.stack([np.kron(eye, np.cos(a1)), np.kron(eye, -np.sin(a1))], axis=1)
    at = 2 * np.pi * np.outer(i2, i1) / N
    Tw1 = np.stack([np.tile(np.cos(at), (1, P)), np.tile(-np.sin(at), (1, P))], axis=1)
    a2 = 2 * np.pi * np.outer(i2, i2) / 128
    F2 = np.stack([np.cos(a2), -np.sin(a2), np.sin(a2)], axis=1)
    Finv2 = np.stack([np.cos(a2), np.sin(a2), -np.sin(a2), np.cos(a2)], axis=1)
    atb = 2 * np.pi * np.outer(i1, i2) / N
    Tw2 = np.stack([np.tile(np.cos(atb), (P, 1)), np.tile(np.sin(atb), (P, 1))], axis=1)
    Finv1bd = np.stack([np.kron(eye, np.cos(a1) / N), np.kron(eye, -np.sin(a1) / N)], axis=1)
    f = lambda x: np.ascontiguousarray(x).astype(np.float32)
    return dict(F1bd=f(F1bd), Tw1=f(Tw1), F2=f(F2), Finv2=f(Finv2), Tw2=f(Tw2), Finv1bd=f(Finv1bd))


def _filter_tables(Lx, D, emb=33):
    f32 = np.float32
    bands_n = (emb - 1) // 2
    t = np.linspace(0.0, 1.0, Lx, dtype=f32)
    ang = (f32(2.0 * math.pi / Lx) * np.arange(Lx, dtype=f32))
    bands = np.linspace(1e-4, bands_n - 1, bands_n, dtype=f32)[None, :]
    feats = np.concatenate([t[:, None], np.cos(bands * ang[:, None]), -np.sin(bands * ang[:, None])], axis=-1).astype(f32)
    idx = np.concatenate([np.arange(Lx), [0], np.arange(Lx - 1, 0, -1)])
    featsT = np.ascontiguousarray(feats[idx].T)
    tpos = t[idx].copy()
    tpos[Lx] = 1e4
    return featsT.astype(f32), _rep(tpos)


def _deltas(D):
    f32 = np.float32
    return np.abs(np.linspace(math.log(1e-2) / 1.5, math.log(1e-2) / 0.3, D, dtype=f32)).astype(f32)


def host_prepare(cfg, inp, b):
    D, KD, DEPTH = cfg.D, cfg.KD, cfg.DEPTH
    f32 = np.float32
    m = {}
    xT = np.concatenate([inp["ctx"][b].T, inp["x"][b].T], axis=1)
    m["xin"] = np.ascontiguousarray(xT).astype(f32)
    pos = np.concatenate([np.zeros((cfg.LC, D), f32), _pos_embed(cfg)], axis=0)
    m["pos"] = np.ascontiguousarray(pos.T)
    cc = np.stack([inp["c"][b], inp["c_ctx"]], axis=1)
    m["cT"] = np.ascontiguousarray(cc.reshape(KD, 128, 2).transpose(1, 0, 2)).astype(f32)
    m["ada_w"] = np.stack([_tile_w(inp["ada_w"][l], 128) for l in range(DEPTH)])
    m["ada_b"] = np.stack([_cols(inp["ada_b"][l]) for l in range(DEPTH)])
    m["g1"] = np.stack([_cols(inp["norm1_g"][l]) for l in range(DEPTH)])
    m["g2"] = np.stack([_cols(inp["norm2_g"][l]) for l in range(DEPTH)])
    m["gfin"] = _cols(inp["final_g"])
    wfm, wtm = [], []
    for l in range(DEPTH):
        W = inp["w_in"][l]
        fm = np.concatenate([W[:, :cfg.XBC], W[:, cfg.OFF_HY:cfg.OFF_GATE], W[:, cfg.OFF_GATE:]], axis=1)
        wfm.append(_tile_w(fm, 128))
        wtm.append(np.concatenate([W[:, cfg.OFF_Z:cfg.OFF_HY], W[:, cfg.OFF_DT:cfg.OFF_Z]], axis=1))
    m["w_fm"] = np.stack(wfm)
    tmb = []
    for l in range(DEPTH):
        W = wtm[l]
        blks = []
        c = 0
        while c < cfg.NTM:
            cw = min(512, cfg.DI - c) if c < cfg.DI else cfg.NTM - c
            blk = np.zeros((D, 512), f32)
            blk[:, :cw] = W[:, c:c + cw]
            blks.append(_tile_w(blk, 512)[0])
            c += cw
        tmb.append(np.stack(blks))
    m["w_tm"] = np.stack(tmb)
    m["mcw"] = np.stack([np.ascontiguousarray(inp["m_conv_w"][l].T.reshape(-1, 128, 5).transpose(1, 0, 2)) for l in range(DEPTH)]).astype(f32)
    m["mcb"] = np.stack([_cols(inp["m_conv_b"][l]) for l in range(DEPTH)])
    m["hcw"] = np.stack([np.ascontiguousarray(inp["h_conv_w"][l].T.reshape(-1, 128, 3).transpose(1, 0, 2)) for l in range(DEPTH)]).astype(f32)
    m["hcb"] = np.stack([_cols(inp["h_conv_b"][l]) for l in range(DEPTH)])
    m["dtb"] = np.stack([_rep(inp["m_dt_bias"][l].reshape(-1)) for l in range(DEPTH)])
    m["alog"] = np.stack([_rep(inp["m_a_log"][l].reshape(-1)) for l in range(DEPTH)])
    m["dskip"] = np.stack([_rep(inp["m_d"][l]) for l in range(DEPTH)])
    m["mng"] = np.stack([_cols(inp["m_norm_g"][l]) for l in range(DEPTH)])
    m["mwo"] = np.stack([_tile_w(inp["m_w_out"][l], 128) for l in range(DEPTH)])
    m["hwo"] = np.stack([_tile_w(inp["h_w_out"][l], 128) for l in range(DEPTH)])
    m["wmo"] = np.stack([_tile_w(inp["w_merge_out"][l], 128) for l in range(DEPTH)])
    m["wg"] = np.stack([_tile_w(inp["ffn_w_gu"][l][:, :cfg.FF], 128) for l in range(DEPTH)])
    m["wu"] = np.stack([_tile_w(inp["ffn_w_gu"][l][:, cfg.FF:], 128) for l in range(DEPTH)])
    m["wd"] = np.stack([_tile_w(inp["ffn_w_down"][l], 128) for l in range(DEPTH)])
    m["hw1"] = np.ascontiguousarray(inp["hf_w1"]).astype(f32)
    m["hw2"] = np.ascontiguousarray(inp["hf_w2"]).astype(f32)
    m["hw3"] = np.ascontiguousarray(inp["hf_w3"]).astype(f32)
    m["hb1"] = np.ascontiguousarray(inp["hf_b1"][:, :, None]).astype(f32)
    m["hb2"] = np.ascontiguousarray(inp["hf_b2"][:, :, None]).astype(f32)
    m["hfr"] = np.ascontiguousarray(inp["hf_freq"][:, :, None]).astype(f32)
    for nm, Lx in (("L", cfg.L), ("C", cfg.LC)):
        N1 = 2 * Lx // 128
        P = 128 // N1
        for k, v in _fft_tables(N1).items():
            m[k + nm] = v
        ft, tp = _filter_tables(Lx, D)
        m["feats" + nm] = ft
        m["tpos" + nm] = tp
        hb = np.stack([np.repeat(inp["h_bias"][l].reshape(-1, P), N1, axis=1).T for l in range(DEPTH)])
        m["hbias" + nm] = np.ascontiguousarray(hb).astype(f32)
    m["ndelta"] = _cols(-_deltas(D))
    i = np.arange(128)
    Uf = (i[:, None] <= i[None, :]).astype(f32)
    Lf = (i[:, None] > i[None, :]).astype(f32)
    Ub = (i[:, None] >= i[None, :]).astype(f32)
    Lb = (i[:, None] < i[None, :]).astype(f32)
    m["masks"] = np.ascontiguousarray(np.stack([Uf, Lf, Ub, Lb, np.ones((128, 128), f32), np.eye(128, dtype=f32)], axis=1))
    return m


class Prog:
    def __init__(self, cfg, shapes):
        self.cfg = cfg
        self.nc = bass.Bass("TRN2", target_bir_lowering=False)
        self.S = Sched(self.nc)
        nc = self.nc
        self.din = {k: nc.dram_tensor(k, list(shp), F32, kind="ExternalInput").ap() for k, shp in shapes.items()}
        c = cfg
        self.out = nc.dram_tensor("out", [c.D, c.L], F32, kind="ExternalOutput").ap()

        def scr(name, shape):
            return nc.dram_tensor(name, list(shape), F32, kind="Internal").ap()

        self.xT = scr("xT", [c.D, c.TT])
        self.PTR = 4096 if c.NFM > 4096 else c.NFM
        self.pT_parts = [scr("pT%d" % i, [min(self.PTR, c.NFM - i * self.PTR), c.TT]) for i in range(-(-c.NFM // self.PTR))]
        self.z_tm = scr("z_tm", [c.TT, c.DI])
        self.dt_tm = scr("dt_tm", [c.TT, 2 * c.H])
        self.xs_tm = scr("xs_tm", [c.TT, c.DI])
        self.B_tm = scr("B_tm", [c.TT, c.GN])
        self.B_fm = scr("B_fm", [c.GN, c.TT])
        self.C_fm = scr("C_fm", [c.GN, c.TT])
        self.yf_tm = scr("yf_tm", [c.TT, c.DI])
        self.ym_fm = scr("ym_fm", [c.DI, c.TT])
        self.hy = {}
        for nm, Lx in (("L", c.L), ("C", c.LC)):
            self.hy[nm] = dict(
                L=Lx, N1=2 * Lx // 128, P=128 // (2 * Lx // 128),
                x0=scr("x0" + nm, [c.D, 2 * Lx]), u=scr("u" + nm, [c.D, 2 * Lx]),
                yh=scr("yh" + nm, [c.D, 2 * Lx]), filt=scr("filt" + nm, [c.D, 2 * Lx]),
                HP=min(512, c.D * (2 * Lx // 128) // 128),
                H=[scr("H%s%d" % (nm, i), [min(512, c.D * (2 * Lx // 128) // 128), 128, 256])
                   for i in range(-(-(c.D * (2 * Lx // 128) // 128) // 512))])
        S = self.S
        self.masks = S.sbuf([128, 6, 128], F32, "masks")
        self.masksr = S.sbuf([128, 6, 128], F32R, "masksr")
        self.mods = [S.sbuf([128, 6 * c.KD, 2], F32, "mods") for _ in range(c.DEPTH)]
        self.A1 = [S.sbuf([128, c.KD, 2], F32, "A1") for _ in range(c.DEPTH)]
        self.A2 = [S.sbuf([128, c.KD, 2], F32, "A2") for _ in range(c.DEPTH)]
        self.cpi = S.sbuf([128, 1], F32, "cpi")

    def pTc(self, cc):
        r0 = cc * 128
        part = self.pT_parts[r0 // self.PTR]
        r1 = r0 % self.PTR
        return part[r1:r1 + 128, :]

    def Hc(self, nm, ps):
        hh = self.hy[nm]
        return hh["H"][ps // hh["HP"]][ps % hh["HP"]]

    def build(self, nstages=None):
        c = self.cfg
        stages = [self.prologue]
        for l in range(c.DEPTH):
            stages += [lambda l=l: self.in_proj(l), lambda l=l: self.conv_stage(l),
                       lambda l=l: self.ssd(l, 0), lambda l=l: self.ssd(l, 1)]
            for nm in ("L", "C"):
                stages += [lambda l=l, nm=nm: self.filter_gen(l, nm),
                           lambda l=l, nm=nm: self.fft_pass(l, nm, True),
                           lambda l=l, nm=nm: self.fft_pass(l, nm, False)]
            stages += [lambda l=l: self.out_proj(l), lambda l=l: self.ffn(l)]
        stages.append(self.final)
        for f in stages[:nstages]:
            f()
        return self.nc

    def prologue(self):
        c, S, din = self.cfg, self.S, self.din
        st = contextlib.ExitStack()
        S.dma(self.masks[:], din["masks"], key="masks", writes=[self.masks])
        S.dma(self.masksr[:], din["masks"], key="masksr", writes=[self.masksr], queue="pool")
        S.memset("dve", self.cpi[:], -math.pi, [self.cpi])
        xa = [S.sbuf([128, c.TT], F32, "xa", st) for _ in range(2)]
        xb = [S.sbuf([128, c.TT], F32, "xb", st) for _ in range(2)]
        for k in range(c.KD):
            a, b_ = xa[k % 2], xb[k % 2]
            S.dma(a[:], din["xin"][k * 128:(k + 1) * 128, :], key=("xa", k % 2), writes=[a])
            S.dma(b_[:], din["pos"][k * 128:(k + 1) * 128, :], key=("xb", k % 2), writes=[b_])
            S.tt("dve", a[:], a[:], b_[:], ALU.add, [a, b_], [a])
            S.dma(self.xT[k * 128:(k + 1) * 128, :], a[:], key=("xa", k % 2), reads=[a])
        for nm in ("L", "C"):
            h = self.hy[nm]
            Lx = h["L"]
            S.memset("pool", xb[0][:, 0:Lx], 0.0, [xb[0]])
            for k in range(c.KD):
                for t in (h["x0"], h["u"]):
                    S.dma(t[k * 128:(k + 1) * 128, Lx:2 * Lx], xb[0][:, 0:Lx], key="zpad", reads=[xb[0]])
        S.flush()
        st.close()
        st = contextlib.ExitStack()
        sc = S.sbuf([128, c.KD, 2], F32, "sc", st)
        S.dma(sc[:], din["cT"], key="sc", writes=[sc])
        S.act(sc[:], sc[:], AF.Silu, [sc], [sc])
        wt = [S.sbuf([128, c.KD, 128], F32, "adaw", st) for _ in range(3)]
        pm = S.psum([128, 6 * c.KD, 2], F32, "pm", st)
        ab = S.sbuf([128, 6 * c.KD], F32, "ab", st)
        g = S.sbuf([128, c.KD], F32, "g", st)
        for l in range(c.DEPTH):
            for cc in range(6 * c.KD):
                w = wt[cc % 3]
                S.dma(w[:], din["ada_w"][l, cc], key=("adaw", cc % 3), writes=[w])
                for k in range(c.KD):
                    S.mm(pm[:, cc, :], w[:, k, :], sc[:, k, :], k == 0, k == c.KD - 1, [w, sc], [pm])
            S.dma(ab[:], din["ada_b"][l], key="ab", writes=[ab])
            md = self.mods[l]
            S.tt("dve", md[:], pm[:], ab[:].unsqueeze(2).to_broadcast([128, 6 * c.KD, 2]), ALU.add, [pm, ab], [md])
            for (A, gname, off) in ((self.A1[l], "g1", c.KD), (self.A2[l], "g2", 4 * c.KD)):
                S.dma(g[:], din[gname][l], key="g", writes=[g])
                S.ts("dve", A[:], md[:, off:off + c.KD, :], 1.0, None, ALU.add, None, [md], [A])
                S.tt("dve", A[:], A[:], g[:].unsqueeze(2).to_broadcast([128, c.KD, 2]), ALU.mult, [A, g], [A])
        S.flush()
        st.close()

    def norm_block(self, st, T, names="n", hdt=F32R):
        c, S = self.cfg, self.S
        xk = [S.sbuf([128, T], F32, "xk" + names, st) for _ in range(3)]
        h = S.sbuf([128, c.KD, T], hdt, "h" + names, st)
        sq = [S.sbuf([128, T], F32R, "sq" + names, st) for _ in range(2)]
        rstd = S.sbuf([128, T], F32, "rstd" + names, st)
        tmp = [S.sbuf([128, T], F32, "ntmp" + names, st) for _ in range(2)]
        nb = (T + 511) // 512
        pst = [S.psum([128, 512], F32, "pst" + names, st) for _ in range(nb)]
        return dict(xk=xk, h=h, sq=sq, rstd=rstd, tmp=tmp, pst=pst, T=T, n=names)

    def norm_run(self, nb_, t0, T, r, A, Bsh, boff):
        c, S = self.cfg, self.S
        xk, h, sq, rstd, tmp, pst = nb_["xk"], nb_["h"], nb_["sq"], nb_["rstd"], nb_["tmp"], nb_["pst"]
        ones = self.masksr[:, 4, :]
        it = 0
        for k in range(c.KD):
            x = xk[it % 3]
            S.dma(x[:, :T], self.xT[k * 128:(k + 1) * 128, t0:t0 + T], key=("nx" + nb_["n"], it % 3), writes=[x])
            it += 1
            s = sq[k % 2]
            S.act(s[:, :T], x[:, :T], AF.Square, [x], [s])
            for j in range((T + 511) // 512):
                w = min(512, T - j * 512)
                S.mm(pst[j][:, :w], ones, s[:, j * 512:j * 512 + w], k == 0, k == c.KD - 1, [s, self.masksr], [pst[j]])
        for j in range((T + 511) // 512):
            w = min(512, T - j * 512)
            S.ts("dve", rstd[:, j * 512:j * 512 + w], pst[j][:, :w], 1.0 / c.D, c.EPS, ALU.mult, ALU.add, [pst[j]], [rstd])
        S.act(rstd[:, :T], rstd[:, :T], AF.Sqrt, [rstd], [rstd])
        S.op("dve", lambda e: e.reciprocal(out=rstd[:, :T], in_=rstd[:, :T]), [rstd], [rstd])
        for k in range(c.KD):
            x = xk[it % 3]
            S.dma(x[:, :T], self.xT[k * 128:(k + 1) * 128, t0:t0 + T], key=("nx" + nb_["n"], it % 3), writes=[x])
            it += 1
            t_ = tmp[k % 2]
            S.stt("dve", t_[:, :T], x[:, :T], A[:, k, r:r + 1], rstd[:, :T], ALU.mult, ALU.mult, [x, A, rstd], [t_])
            if Bsh is None:
                S.cp("act", h[:, k, :T], t_[:, :T], [t_], [h])
            else:
                S.act(h[:, k, :T], t_[:, :T], AF.Identity, [t_, Bsh], [h], bias=Bsh[:, boff + k, r:r + 1], scale=1.0)

    def in_proj(self, l):
        c, S, din = self.cfg, self.S, self.din
        st = contextlib.ExitStack()
        T = c.TB
        nbk = self.norm_block(st, T)
        h = nbk["h"]
        wf = [S.sbuf([128, c.KD, 128], F32R, "wf", st) for _ in range(3)]
        wtm = [S.sbuf([128, c.KD, 512], F32R, "wtm", st) for _ in range(2)]
        nb = T // 512 if T >= 512 else 1
        pp = [[S.psum([128, 512], F32, "pp", st) for _ in range(nb)] for _ in range(2)]
        ob = [S.sbuf([128, T], F32, "ob", st) for _ in range(2)]
        otm = [S.sbuf([128, 512], F32, "otm", st) for _ in range(2)]
        ngate0 = (c.XBC + 3 * c.D) // 128
        ntmb = din["w_tm"].shape[1]
        it = 0
        for (t0, Tb, r) in c.blocks(T):
            self.norm_run(nbk, t0, Tb, r, self.A1[l], self.mods[l], 0)
            for cc in range(c.NFM // 128):
                w = wf[cc % 3]
                S.dma(w[:], din["w_fm"][l, cc], key=("wf", cc % 3), writes=[w], queue="pool")
                o = ob[cc % 2]
                for j in range((Tb + 511) // 512):
                    wd = min(512, Tb - j * 512)
                    p = pp[cc % 2][j]
                    for k in range(c.KD):
                        S.mm(p[:, :wd], w[:, k, :], h[:, k, j * 512:j * 512 + wd], k == 0, k == c.KD - 1, [w, h], [p])
                    if cc >= ngate0:
                        S.act(o[:, j * 512:j * 512 + wd], p[:, :wd], AF.Sigmoid, [p], [o])
                    elif (cc + j) % 2 == 0:
                        S.cp("act", o[:, j * 512:j * 512 + wd], p[:, :wd], [p], [o])
                    else:
                        S.cp("dve", o[:, j * 512:j * 512 + wd], p[:, :wd], [p], [o])
                S.dma(self.pTc(cc)[:, t0:t0 + Tb], o[:, :Tb], key=("ob", cc % 2), reads=[o])
            for cb in range(ntmb):
                c0 = cb * 512
                isdt = c0 >= c.DI
                cw = (c.NTM - c.DI) if isdt else min(512, c.DI - c0)
                w = wtm[cb % 2]
                S.dma(w[:], din["w_tm"][l, cb], key=("wtm", cb % 2), writes=[w], queue="pool")
                for tc in range(Tb // 128):
                    p = pp[it % 2][0]
                    o = otm[it % 2]
                    it += 1
                    for k in range(c.KD):
                        S.mm(p[:, :cw], h[:, k, tc * 128:(tc + 1) * 128], w[:, k, :cw], k == 0, k == c.KD - 1, [w, h], [p])
                    if isdt:
                        S.cp("dve", o[:, :cw], p[:, :cw], [p], [o])
                        S.dma(self.dt_tm[t0 + tc * 128:t0 + (tc + 1) * 128, :], o[:, :cw], key=("otm", id(o)), reads=[o])
                    else:
                        S.act(o[:, :cw], p[:, :cw], AF.Silu, [p], [o])
                        S.dma(self.z_tm[t0 + tc * 128:t0 + (tc + 1) * 128, c0:c0 + cw], o[:, :cw], key=("otm", id(o)), reads=[o])
        S.flush()
        st.close()

    def _conv(self, eng, acc, raw, wt, bt, ci, K):
        c, S = self.cfg, self.S
        half = K // 2
        S.ts(eng, acc[:], raw[:], wt[:, ci, half:half + 1], bt[:, ci:ci + 1], ALU.mult, ALU.add, [raw, wt, bt], [acc])
        for j in range(K):
            sh = j - half
            if sh == 0:
                continue
            for (s0, s1) in ((0, c.LC), (c.LC, c.TT)):
                lo, hi = max(s0, s0 - sh), min(s1, s1 - sh)
                S.stt(eng, acc[:, lo:hi], raw[:, lo + sh:hi + sh], wt[:, ci, j:j + 1], acc[:, lo:hi], ALU.mult, ALU.add, [raw, wt, acc], [acc])

    def conv_stage(self, l):
        c, S, din = self.cfg, self.S, self.din
        st = contextlib.ExitStack()
        nxc = c.XBC // 128
        mcw = S.sbuf([128, nxc, 5], F32, "mcw", st)
        mcb = S.sbuf([128, nxc], F32, "mcb", st)
        hcw = S.sbuf([128, 3 * c.KD, 3], F32, "hcw", st)
        hcb = S.sbuf([128, 3 * c.KD], F32, "hcb", st)
        S.dma(mcw[:], din["mcw"][l], key="mcw", writes=[mcw])
        S.dma(mcb[:], din["mcb"][l], key="mcb", writes=[mcb])
        S.dma(hcw[:], din["hcw"][l], key="hcw", writes=[hcw])
        S.dma(hcb[:], din["hcb"][l], key="hcb", writes=[hcb])
        raw = [S.sbuf([128, c.TT], F32, "raw", st) for _ in range(2)]
        acc = [S.sbuf([128, c.TT], F32, "acc", st) for _ in range(2)]
        ptr = [S.psum([128, 4, 128], F32, "ptr", st) for _ in range(2)]
        tmo = [S.sbuf([128, 4, 128], F32, "tmo", st) for _ in range(2)]
        ident = self.masks[:, 5, :]
        nch = c.TT // 128
        it = 0
        for cc in range(nxc):
            rw, ac = raw[cc % 2], acc[cc % 2]
            eng = "dve"
            S.dma(rw[:], self.pTc(cc), key=("raw", cc % 2), writes=[rw], queue="pool")
            self._conv(eng, ac, rw, mcw, mcb, cc, 5)
            S.act(ac[:], ac[:], AF.Silu, [ac], [ac])
            nxs = c.DI // 128
            nb = c.GN // 128
            if cc < nxs + nb:
                dst = self.xs_tm if cc < nxs else self.B_tm
                col0 = (cc if cc < nxs else cc - nxs) * 128
                dv = dst.rearrange("(a p) q -> p a q", p=128)
                for q0 in range(0, nch, 4):
                    nq = min(4, nch - q0)
                    p, o = ptr[it % 2], tmo[it % 2]
                    it += 1
                    for q in range(nq):
                        S.tr(p[:, q, :], ac[:, (q0 + q) * 128:(q0 + q + 1) * 128], ident, [ac, self.masks], [p])
                    S.cp("act" if it % 2 else "dve", o[:, :nq, :], p[:, :nq, :], [p], [o])
                    S.dma(dv[:, q0:q0 + nq, col0:col0 + 128], o[:, :nq, :], key=("tmo", id(o)), reads=[o])
            if nxs <= cc < nxs + nb:
                S.dma(self.B_fm[(cc - nxs) * 128:(cc - nxs + 1) * 128, :], ac[:], key=("accst", cc % 2), reads=[ac])
            elif cc >= nxs + nb:
                S.dma(self.C_fm[(cc - nxs - nb) * 128:(cc - nxs - nb + 1) * 128, :], ac[:], key=("accst", cc % 2), reads=[ac])
        for i in range(c.KD):
            for (j, b) in ((1, 0), (2, 1)):
                cc = nxc + j * c.KD + i
                S.dma(raw[b][:], self.pTc(cc), key=("raw", b), writes=[raw[b]], queue="pool")
                self._conv("dve", acc[b], raw[b], hcw, hcb, j * c.KD + i, 3)
            S.tt("dve", acc[0][:], acc[0][:], acc[1][:], ALU.mult, [acc[0], acc[1]], [acc[0]])
            for (nm, s0, s1) in (("C", 0, c.LC), ("L", c.LC, c.TT)):
                S.dma(self.hy[nm]["u"][i * 128:(i + 1) * 128, 0:s1 - s0], acc[0][:, s0:s1], key=("accst", 0), reads=[acc[0]])
            cc = nxc + i
            S.dma(raw[1][:], self.pTc(cc), key=("raw", 1), writes=[raw[1]], queue="pool")
            self._conv("dve", acc[1], raw[1], hcw, hcb, i, 3)
            for (nm, s0, s1) in (("C", 0, c.LC), ("L", c.LC, c.TT)):
                S.dma(self.hy[nm]["x0"][i * 128:(i + 1) * 128, 0:s1 - s0], acc[1][:, s0:s1], key=("accst", 1), reads=[acc[1]])
        S.flush()
        st.close()

    def ssd(self, l, d):
        c, S, din = self.cfg, self.S, self.din
        st = contextlib.ExitStack()
        H, G, R, DI, GN, HB = c.H, c.G, c.R, c.DI, c.GN, c.HB
        RW = R * 64
        U = self.masks[:, 0 if d == 0 else 2, :]
        Lm = self.masks[:, 1 if d == 0 else 3, :]
        Lmr = self.masksr[:, 1 if d == 0 else 3, :]
        ones = self.masks[:, 4, :]
        ident = self.masks[:, 5, :]
        dtb = S.sbuf([128, H], F32, "dtb", st)
        Abc = S.sbuf([128, H], F32, "Abc", st)
        S.dma(dtb[:], din["dtb"][l][:, d * H:(d + 1) * H], key="dtb", writes=[dtb])
        S.dma(Abc[:], din["alog"][l][:, d * H:(d + 1) * H], key="Abc", writes=[Abc])
        S.act(Abc[:], Abc[:], AF.Exp, [Abc], [Abc])
        S.ts("dve", Abc[:], Abc[:], -1.0, None, ALU.mult, None, [Abc], [Abc])
        St = S.sbuf([128, G, RW], F32, "St", st)
        Sr = S.sbuf([128, G, RW], F32R, "Sr", st)
        S.memset("dve", St[:], 0.0, [(St, g) for g in range(G)])
        S.cp("dve", Sr[:], St[:], [(St, g) for g in range(G)], [(Sr, g) for g in range(G)])
        xs = [S.sbuf([128, DI], F32, "xs", st)] * 2
        Btm = [S.sbuf([128, GN], F32R, "Btm", st) for _ in range(2)]
        Bfm = [S.sbuf([128, G, 128], F32R, "Bfm", st) for _ in range(2)]
        Cfm = [S.sbuf([128, G, 128], F32R, "Cfm", st) for _ in range(2)]
        dtr = [S.sbuf([128, H], F32, "dtr", st) for _ in range(2)]
        dt = S.sbuf([128, H], F32, "dt", st)
        dtA = S.sbuf([128, H], F32, "dtA", st)
        eall = S.sbuf([128, 3, H], F32, "eall", st)
        xd = S.sbuf([128, DI], F32R, "xd", st)
        xdd = S.sbuf([128, DI], F32R, "xdd", st)
        cbm = S.sbuf([128, G, 128], F32, "cbm", st)
        rhsb = [S.sbuf([128, HB, 128], F32R, "rhsb", st) for _ in range(2)]
        E = [S.sbuf([128, HB, 128], F32, "E", st) for _ in range(2)]
        MT = [S.sbuf([128, HB, 128], F32R, "MT", st) for _ in range(2)]
        ysb = S.sbuf([128, DI], F32, "ysb", st)
        ytmp = [S.sbuf([128, RW], F32, "ytmp", st) for _ in range(2)]
        ps_small = S.psum([128, 3, H], F32, "pssm", st)
        ps_cb = S.psum([128, G, 128], F32, "pscb", st)
        ps_seg = S.psum([128, HB, 128], F32, "psseg", st)
        ps_ys = [S.psum([128, RW], F32, "psy", st) for _ in range(2)]
        ps_yo = S.psum([128, RW], F32, "psyo", st)
        ps_st = S.psum([128, RW], F32, "psst", st)
        if d == 1:
            yf = S.sbuf([128, DI], F32, "yf", st)
            zs = S.sbuf([128, DI], F32, "zs", st)
            dsk = S.sbuf([128, H], F32, "dsk", st)
            mng = S.sbuf([128, DI // 128], F32, "mng", st)
            S.dma(dsk[:], din["dskip"][l], key="dsk", writes=[dsk])
            S.dma(mng[:], din["mng"][l], key="mng", writes=[mng])
            sqt = yf
            ssg = S.sbuf([128, G], F32, "ssg", st)
            ps_tr = ps_cb
            ymT = S.sbuf([128, DI // 128, 128], F32, "ymT", st)
        nch = c.TT // 128
        nctx = c.LC // 128
        if d == 0:
            order = list(range(nch))
        else:
            order = list(range(nctx - 1, -1, -1)) + list(range(nch - 1, nctx - 1, -1))
        for ci, ch in enumerate(order):
            t0 = ch * 128
            b = ci % 2
            x_, Bt, Bf, Cf, dr = xs[b], Btm[b], Bfm[b], Cfm[b], dtr[b]
            S.dma(x_[:], self.xs_tm[t0:t0 + 128, :], key=("xs", 0), writes=[x_])
            S.dma(dr[:], self.dt_tm[t0:t0 + 128, d * H:(d + 1) * H], key=("dtr", b), writes=[dr])
            S.dma(Bt[:], self.B_tm[t0:t0 + 128, :], key=("Btm", b), writes=[Bt], queue="pool")
            S.dma(Bf[:], self.B_fm[:, t0:t0 + 128].rearrange("(g n) t -> n g t", n=128), key=("Bfm", b), writes=[Bf], queue="pool")
            S.dma(Cf[:], self.C_fm[:, t0:t0 + 128].rearrange("(g n) t -> n g t", n=128), key=("Cfm", b), writes=[Cf], queue="pool")
            if d == 1:
                S.dma(yf[:], self.yf_tm[t0:t0 + 128, :], key="yf", writes=[yf])
                S.dma(zs[:], self.z_tm[t0:t0 + 128, :], key="zs", writes=[zs])
            S.tt("dve", dt[:], dr[:], dtb[:], ALU.add, [dr, dtb], [dt])
            S.act(dt[:], dt[:], AF.Exp, [dt], [dt])
            S.act(dt[:], dt[:], AF.Ln, [dt], [dt], bias=1.0, scale=1.0)
            S.tt("dve", dtA[:], dt[:], Abc[:], ALU.mult, [dt, Abc], [dtA])
            S.mm(ps_small[:, 0, :], U, dtA[:], True, True, [dtA, self.masks], [ps_small])
            S.mm(ps_small[:, 1, :], Lm, dtA[:], True, True, [dtA, self.masks], [ps_small])
            S.mm(ps_small[:, 2, :], ones, dtA[:], True, True, [dtA, self.masks], [ps_small])
            S.act(eall[:], ps_small[:], AF.Exp, [ps_small], [eall])
            S.tt("dve", xd[:].rearrange("p (h q) -> p h q", q=64), x_[:].rearrange("p (h q) -> p h q", q=64),
                 dt[:].unsqueeze(2).to_broadcast([128, H, 64]), ALU.mult, [x_, dt], [xd])
            S.tt("pool", xdd[:].rearrange("p (h q) -> p h q", q=64), xd[:].bitcast(F32).rearrange("p (h q) -> p h q", q=64),
                 eall[:, 1, :].unsqueeze(2).to_broadcast([128, H, 64]), ALU.mult, [xd, eall], [xdd])
            for g in range(G):
                S.mm(ps_cb[:, g, :], Bf[:, g, :], Cf[:, g, :], True, True, [Bf, Cf], [ps_cb])
            S.tt("dve", cbm[:], ps_cb[:], U.unsqueeze(1).to_broadcast([128, G, 128]), ALU.mult, [ps_cb, self.masks], [cbm])
            it = 0
            for g in range(G):
                ps_y = ps_ys[g % 2]
                for hb in range(R // HB):
                    h0 = g * R + hb * HB
                    rb, Eb, Mb = rhsb[it % 2], E[it % 2], MT[it % 2]
                    it += 1
                    S.tt("pool", rb[:], U.unsqueeze(1).to_broadcast([128, HB, 128]),
                         dtA[:, h0:h0 + HB].unsqueeze(2).to_broadcast([128, HB, 128]), ALU.mult, [dtA, self.masks], [rb])
                    for j in range(HB):
                        S.mm(ps_seg[:, j, :], Lmr, rb[:, j, :], True, True, [rb, self.masksr], [ps_seg])
                    S.act(Eb[:], ps_seg[:], AF.Exp, [ps_seg], [Eb])
                    S.tt("dve", Mb[:], Eb[:], cbm[:, g, :].unsqueeze(1).to_broadcast([128, HB, 128]), ALU.mult, [Eb, cbm], [Mb])
                    for j in range(HB):
                        hh = h0 + j
                        S.mm(ps_y[:, (hb * HB + j) * 64:(hb * HB + j + 1) * 64], Mb[:, j, :], xd[:, hh * 64:(hh + 1) * 64],
                             True, True, [Mb, xd], [ps_y])
                S.mm(ps_yo[:], Cf[:, g, :], Sr[:, g, :], True, True, [Cf, (Sr, g)], [ps_yo])
                yt = ytmp[g % 2]
                S.tt("dve", yt[:].rearrange("p (h q) -> p h q", q=64), ps_yo[:].rearrange("p (h q) -> p h q", q=64),
                     eall[:, 0, g * R:(g + 1) * R].unsqueeze(2).to_broadcast([128, R, 64]), ALU.mult, [ps_yo, eall], [yt])
                S.tt("dve", ysb[:, g * RW:(g + 1) * RW], yt[:], ps_y[:], ALU.add, [yt, ps_y], [(ysb, g)])
                S.mm(ps_st[:], Bt[:, g * 128:(g + 1) * 128], xdd[:, g * RW:(g + 1) * RW], True, True, [Bt, xdd], [ps_st])
                S.tt("pool", St[:, g, :].rearrange("p (h q) -> p h q", q=64), St[:, g, :].rearrange("p (h q) -> p h q", q=64),
                     eall[:, 2, g * R:(g + 1) * R].unsqueeze(2).to_broadcast([128, R, 64]), ALU.mult, [(St, g), eall], [(St, g)])
                S.tt("dve", St[:, g, :], St[:, g, :], ps_st[:], ALU.add, [(St, g), ps_st], [(St, g)])
                S.cp("act", Sr[:, g, :], St[:, g, :], [(St, g)], [(Sr, g)])
            yall = [(ysb, g) for g in range(G)]
            if d == 0:
                S.dma(self.yf_tm[t0:t0 + 128, :], ysb[:], key="ysb", reads=yall)
            else:
                S.tt("dve", ysb[:], ysb[:], yf[:], ALU.add, yall + [yf], yall)
                S.tt("pool", yf[:].rearrange("p (h q) -> p h q", q=64), x_[:].rearrange("p (h q) -> p h q", q=64),
                     dsk[:].unsqueeze(2).to_broadcast([128, H, 64]), ALU.mult, [x_, dsk, yf], [yf])
                S.tt("dve", ysb[:], ysb[:], yf[:], ALU.add, yall + [yf], yall)
                S.tt("pool", ysb[:], ysb[:], zs[:], ALU.mult, yall + [zs], yall)
                S.act(sqt[:], ysb[:], AF.Square, yall + [sqt], [sqt])
                S.op("dve", lambda e: e.tensor_reduce(out=ssg[:], in_=sqt[:].rearrange("p (g q) -> p g q", g=G), axis=AX.X, op=ALU.add), [sqt], [ssg])
                S.ts("dve", ssg[:], ssg[:], 1.0 / (DI // G), c.EPS, ALU.mult, ALU.add, [ssg], [ssg])
                S.act(ssg[:], ssg[:], AF.Sqrt, [ssg], [ssg])
                S.op("dve", lambda e: e.reciprocal(out=ssg[:], in_=ssg[:]), [ssg], [ssg])
                S.tt("dve", ysb[:].rearrange("p (g q) -> p g q", g=G), ysb[:].rearrange("p (g q) -> p g q", g=G),
                     ssg[:].unsqueeze(2).to_broadcast([128, G, DI // G]), ALU.mult, yall + [ssg], yall)
                for q0 in range(0, DI // 128, 4):
                    for q in range(4):
                        S.tr(ps_tr[:, q, :], ysb[:, (q0 + q) * 128:(q0 + q + 1) * 128], ident, yall + [self.masks], [ps_tr])
                    for q in range(4):
                        S.act(ymT[:, q0 + q, :], ps_tr[:, q, :], AF.Identity, [ps_tr, mng], [ymT], scale=mng[:, q0 + q:q0 + q + 1])
                S.dma(self.ym_fm[:, t0:t0 + 128].rearrange("(a p) t -> p a t", p=128), ymT[:], key="ymT", reads=[ymT])
        S.flush()
        st.close()

    def _sin(self, a_, out, ki, kf, w=None):
        S = self.S
        S.ts("dve", a_[:], a_[:], 1.0 / (2.0 * math.pi), 64.5, ALU.mult, ALU.add, [a_], [a_])
        S.cp("dve", ki[:], a_[:], [a_], [ki])
        S.cp("dve", kf[:], ki[:], [ki], [kf])
        S.tt("dve", a_[:], a_[:], kf[:], ALU.subtract, [a_, kf], [a_])
        S.ts("dve", kf[:], a_[:], 0.0, None, ALU.is_lt, None, [a_], [kf])
        S.tt("dve", a_[:], a_[:], kf[:], ALU.add, [a_, kf], [a_])
        S.ts("dve", a_[:], a_[:], 0.0, 0.999999, ALU.max, ALU.min, [a_], [a_])
        S.act(out, a_[:], AF.Sin, [a_, self.cpi], w or [a_], bias=self.cpi[0:64, 0:1], scale=2.0 * math.pi)

    def filter_gen(self, l, nm):
        c, S, din = self.cfg, self.S, self.din
        hh = self.hy[nm]
        Lx = hh["L"]
        N2 = 2 * Lx
        bs = min(512, Lx)
        nblk = N2 // bs
        st = contextlib.ExitStack()
        w1 = S.sbuf([33, 64], F32, "w1", st)
        w2 = S.sbuf([64, 64], F32, "w2", st)
        w3 = S.sbuf([64, 2 * c.D], F32, "w3", st)
        b1 = S.sbuf([64, 1], F32, "b1", st)
        b2 = S.sbuf([64, 1], F32, "b2", st)
        fr = S.sbuf([64, 1], F32, "fr", st)
        nd = S.sbuf([128, c.KD], F32, "nd", st)
        for (t, k) in ((w1, "hw1"), (w2, "hw2"), (w3, "hw3"), (b1, "hb1"), (b2, "hb2"), (fr, "hfr")):
            S.dma(t[:], din[k][l], key=k, writes=[t])
        S.dma(nd[:], din["ndelta"], key="nd", writes=[nd])
        hid = S.sbuf([64, N2], F32, "hid", st)
        ft = [S.sbuf([33, bs], F32, "ft", st) for _ in range(2)]
        a1 = [S.sbuf([64, bs], F32, "a1", st) for _ in range(2)]
        ph = [S.psum([64, bs], F32, "ph", st) for _ in range(2)]
        ki = S.sbuf([64, bs], I32, "ki", st)
        kf = S.sbuf([64, bs], F32, "kf", st)
        for nb in range(nblk):
            f_, a_, p_ = ft[nb % 2], a1[nb % 2], ph[nb % 2]
            S.dma(f_[:], din["feats" + nm][:, nb * bs:(nb + 1) * bs], key=("ft", nb % 2), writes=[f_])
            S.mm(p_[:], w1[:], f_[:], True, True, [w1, f_], [p_])
            S.ts("dve", a_[:], p_[:], b1[:, 0:1], fr[:, 0:1], ALU.add, ALU.mult, [p_, b1, fr], [a_])
            self._sin(a_, a_[:], ki, kf)
            S.mm(p_[:], w2[:], a_[:], True, True, [w2, a_], [p_])
            S.ts("dve", a_[:], p_[:], b2[:, 0:1], fr[:, 0:1], ALU.add, ALU.mult, [p_, b2, fr], [a_])
            self._sin(a_, hid[:, nb * bs:(nb + 1) * bs], ki, kf, [hid])
        filt = [S.sbuf([128, N2], F32, "filt", st)]
        tp = [S.sbuf([128, bs], F32, "tp", st) for _ in range(2)]
        win = [S.sbuf([128, bs], F32, "win", st) for _ in range(2)]
        pf = [S.psum([128, bs], F32, "pf", st) for _ in range(2)]
        ssb = S.sbuf([128, nblk], F32, "ssb", st)
        ss = S.sbuf([128, 1], F32, "ss", st)
        for i in range(c.KD):
            fl = filt[0]
            S.memset("pool", ssb[:], 0.0, [ssb])
            for nb in range(nblk):
                p_, w_, t_ = pf[nb % 2], win[nb % 2], tp[nb % 2]
                half = 0 if (nb * bs) < Lx else 1
                S.mm(p_[:], w3[:, half * c.D + i * 128:half * c.D + (i + 1) * 128], hid[:, nb * bs:(nb + 1) * bs], True, True, [w3, hid], [p_])
                S.dma(t_[:], din["tpos" + nm][:, nb * bs:(nb + 1) * bs], key=("tp", nb % 2), writes=[t_])
                S.act(w_[:], t_[:], AF.Exp, [t_, nd], [w_], scale=nd[:, i:i + 1])
                S.tt("dve", fl[:, nb * bs:(nb + 1) * bs], p_[:], w_[:], ALU.mult, [p_, w_], [fl])
                S.act(w_[:], fl[:, nb * bs:(nb + 1) * bs], AF.Square, [fl], [w_, ssb], accum=ssb[:, nb:nb + 1])
            S.op("dve", lambda e: e.tensor_reduce(out=ss[:], in_=ssb[:], axis=AX.X, op=ALU.add), [ssb], [ss])
            S.ts("dve", ss[:], ss[:], c.EPS, None, ALU.add, None, [ss], [ss])
            S.act(ss[:], ss[:], AF.Sqrt, [ss], [ss])
            S.op("dve", lambda e: e.reciprocal(out=ss[:], in_=ss[:]), [ss], [ss])
            S.ts("dve", fl[:], fl[:], ss[:, 0:1], None, ALU.mult, None, [fl, ss], [fl])
            S.dma(hh["filt"][i * 128:(i + 1) * 128, :], fl[:], key=("filt", 0), reads=[fl])
        S.flush()
        st.close()

    def _cmul(self, X, Tre, Tim, P_, Q_, out, rT):
        S = self.S
        S.tt("dve", P_[:], X[:], Tre, ALU.mult, [X] + rT, [P_])
        S.tt("dve", Q_[:], X[:], Tim, ALU.mult, [X] + rT, [Q_])
        S.tt("pool", out[:, :, 0, :], P_[:, :, 0, :], Q_[:, :, 1, :], ALU.subtract, [P_, Q_], [(out, 0)])
        S.tt("pool", out[:, :, 1, :], Q_[:, :, 0, :], P_[:, :, 1, :], ALU.add, [P_, Q_], [(out, 1)])

    def fft_pass(self, l, nm, is_filter):
        c, S, din = self.cfg, self.S, self.din
        hh = self.hy[nm]
        Lx, N1, P = hh["L"], hh["N1"], hh["P"]
        npass = c.D // P
        PG = 4
        assert npass % PG == 0 and hh["HP"] % PG == 0
        st = contextlib.ExitStack()
        tabs = {}
        for k, shp in (("F1bd", [128, 2, 128]), ("F2", [128, 3, 128]), ("Finv2", [128, 4, 128]), ("Finv1bd", [128, 2, 128])):
            tabs[k] = S.sbuf(shp, F32R, k, st)
            S.dma(tabs[k][:], din[k + nm], key=k, writes=[tabs[k]], queue="pool")
        for k in ("Tw1", "Tw2"):
            tabs[k] = S.sbuf([128, 2, 128], F32, k, st)
            S.dma(tabs[k][:], din[k + nm], key=k, writes=[tabs[k]])
        hbp = S.sbuf([128, npass], F32, "hbp", st)
        S.dma(hbp[:], din["hbias" + nm][l], key="hbp", writes=[hbp])
        NB = 5
        shp4 = [128, PG, 2, 128]
        u = [S.sbuf([128, PG, 128], F32R, "u", st) for _ in range(NB)]
        x0 = [S.sbuf([128, PG, 128], F32, "x0", st) for _ in range(NB)]
        Ht = [S.sbuf(shp4, F32, "Ht", st) for _ in range(NB)]
        Pt = [[S.sbuf(shp4, F32, "Pt", st) for _ in range(2)] for _ in range(3)]
        Qt = [[S.sbuf(shp4, F32, "Qt", st) for _ in range(2)] for _ in range(3)]
        Ap = [S.sbuf(shp4, F32R, "Ap", st) for _ in range(2)]
        Yt = [S.sbuf(shp4, F32R, "Yt", st) for _ in range(2)]
        Cp = [S.sbuf(shp4, F32R, "Cp", st) for _ in range(2)]
        yo = [S.sbuf([128, PG, 128], F32, "yo", st) for _ in range(2)]
        psA = S.psum(shp4, F32, "psA", st)
        psX = S.psum(shp4, F32, "psX", st)
        psC = S.psum(shp4, F32, "psC", st)
        psY = S.psum([128, PG, 128], F32, "psY", st)
        src = hh["filt"] if is_filter else hh["u"]
        F1, F2, Fi2, Fi1 = tabs["F1bd"], tabs["F2"], tabs["Finv2"], tabs["Finv1bd"]
        bc = lambda t, i: t[:, i, :].unsqueeze(1).unsqueeze(1).to_broadcast(shp4)
        ngrp = npass // PG

        def vw(t, gi):
            c0 = gi * PG * P
            return t[c0:c0 + PG * P, :].rearrange("(g c) (a b) -> (c a) g b", g=PG, b=128)

        def hslice(gi):
            ps0 = gi * PG
            return hh["H"][ps0 // hh["HP"]][ps0 % hh["HP"]:ps0 % hh["HP"] + PG].rearrange("g p (r k) -> p g r k", r=2)

        def st1(gi):
            n, b = gi % NB, gi % 2
            S.dma(u[n][:], vw(src, gi), key=("u", n), writes=[u[n]], queue="pool")
            if not is_filter:
                S.dma(Ht[n][:], hslice(gi), key=("Ht", n), writes=[Ht[n]])
                S.dma(x0[n][:], vw(hh["x0"], gi), key=("x0", n), writes=[x0[n]])
            for g in range(PG):
                S.mm(psA[:, g, :, :], u[n][:, g, :], F1[:], True, True, [u[n], F1], [psA])
            self._cmul(psA, bc(tabs["Tw1"], 0), bc(tabs["Tw1"], 1), Pt[0][b], Qt[0][b], Ap[b], [tabs["Tw1"]])

        def st2(gi):
            n, b = gi % NB, gi % 2
            apr = [(Ap[b], 0), (Ap[b], 1)]
            for g in range(PG):
                S.mm(psX[:, g, :, :], F2[:, 0, :], Ap[b][:, g, :, :], True, False, apr + [F2], [psX])
                S.mm(psX[:, g, 0, :], F2[:, 2, :], Ap[b][:, g, 1, :], False, False, apr + [F2], [psX])
                S.mm(psX[:, g, 1, :], F2[:, 1, :], Ap[b][:, g, 0, :], False, True, apr + [F2], [psX])
            if is_filter:
                S.cp("act", Ht[n][:], psX[:], [psX], [Ht[n]])
                S.dma(hslice(gi), Ht[n][:], key=("Ht", n), reads=[Ht[n]])
                return
            hre = Ht[n][:, :, 0, :].unsqueeze(2).to_broadcast(shp4)
            him = Ht[n][:, :, 1, :].unsqueeze(2).to_broadcast(shp4)
            self._cmul(psX, hre, him, Pt[1][b], Qt[1][b], Yt[b], [Ht[n]])

        def st3(gi):
            b = gi % 2
            ytr = [(Yt[b], 0), (Yt[b], 1)]
            for g in range(PG):
                S.mm(psC[:, g, :, :], Yt[b][:, g, 0, :], Fi2[:, 0:2, :], True, False, ytr + [Fi2], [psC])
                S.mm(psC[:, g, :, :], Yt[b][:, g, 1, :], Fi2[:, 2:4, :], False, True, ytr + [Fi2], [psC])
            self._cmul(psC, bc(tabs["Tw2"], 0), bc(tabs["Tw2"], 1), Pt[2][b], Qt[2][b], Cp[b], [tabs["Tw2"]])

        def st4(gi):
            n, b = gi % NB, gi % 2
            ps0 = gi * PG
            cpr = [(Cp[b], 0), (Cp[b], 1)]
            for g in range(PG):
                S.mm(psY[:, g, :], Fi1[:, 0, :], Cp[b][:, g, 0, :], True, False, cpr + [Fi1], [psY])
                S.mm(psY[:, g, :], Fi1[:, 1, :], Cp[b][:, g, 1, :], False, True, cpr + [Fi1], [psY])
            S.tt("pool", yo[b][:], u[n][:].bitcast(F32), hbp[:, ps0:ps0 + PG].unsqueeze(2).to_broadcast([128, PG, 128]), ALU.mult,
                 [u[n], hbp], [yo[b]])
            S.tt("dve", yo[b][:], yo[b][:], psY[:], ALU.add, [yo[b], psY], [yo[b]])
            S.tt("pool", yo[b][:], yo[b][:], x0[n][:], ALU.mult, [yo[b], x0[n]], [yo[b]])
            S.dma(vw(hh["yh"], gi), yo[b][:], key=("yo", b), reads=[yo[b]])

        stages = [st1, st2] if is_filter else [st1, st2, st3, st4]
        for s_ in range(ngrp + len(stages) - 1):
            for k, fn in enumerate(stages):
                gi = s_ - k
                if 0 <= gi < ngrp:
                    fn(gi)
        S.flush()
        st.close()

    def out_proj(self, l):
        c, S, din = self.cfg, self.S, self.din
        st = contextlib.ExitStack()
        T = c.TF
        KM = c.DI // 128
        ym = S.sbuf([128, KM, T], F32R, "ym", st)
        yh = S.sbuf([128, c.KD, T], F32R, "yh", st)
        mg = S.sbuf([128, c.KD, T], F32R, "mg", st)
        wm = [S.sbuf([128, KM, 128], F32R, "wm", st) for _ in range(2)]
        wh = [S.sbuf([128, c.KD, 128], F32R, "wh", st) for _ in range(2)]
        ga = [S.sbuf([128, T], F32, "ga", st) for _ in range(2)]
        gb = [S.sbuf([128, T], F32, "gb", st) for _ in range(2)]
        t1 = [S.sbuf([128, T], F32, "t1", st) for _ in range(2)]
        t2 = [S.sbuf([128, T], F32, "t2", st) for _ in range(2)]
        xo = [S.sbuf([128, T], F32, "xo", st) for _ in range(2)]
        pP = [S.psum([128, T], F32, "pP", st) for _ in range(2)]
        pQ = [S.psum([128, T], F32, "pQ", st) for _ in range(2)]
        pZ = [S.psum([128, T], F32, "pZ", st) for _ in range(2)]
        g0 = (c.XBC + 3 * c.D)
        md = self.mods[l]
        for (t0, Tb, r) in c.blocks(T):
            S.dma(ym[:, :, :Tb], self.ym_fm[:, t0:t0 + Tb].rearrange("(k p) t -> p k t", p=128), key="ym", writes=[ym], queue="pool")
            hh = self.hy["C" if r == 1 else "L"]
            ts0 = t0 if r == 1 else t0 - c.LC
            S.dma(yh[:, :, :Tb], hh["yh"][:, ts0:ts0 + Tb].rearrange("(k p) t -> p k t", p=128), key="yh", writes=[yh], queue="pool")
            for cc in range(c.KD):
                b = cc % 2
                S.dma(wm[b][:], din["mwo"][l, cc], key=("wm", b), writes=[wm[b]], queue="pool")
                S.dma(wh[b][:], din["hwo"][l, cc], key=("wh", b), writes=[wh[b]], queue="pool")
                S.dma(ga[b][:, :Tb], self.pTc(g0 // 128 + cc)[:, t0:t0 + Tb], key=("ga", b), writes=[ga[b]])
                S.dma(gb[b][:, :Tb], self.pTc((g0 + c.D) // 128 + cc)[:, t0:t0 + Tb], key=("gb", b), writes=[gb[b]])
                for k in range(KM):
                    S.mm(pP[b][:, :Tb], wm[b][:, k, :], ym[:, k, :Tb], k == 0, k == KM - 1, [wm[b], ym], [pP[b]])
                for k in range(c.KD):
                    S.mm(pQ[b][:, :Tb], wh[b][:, k, :], yh[:, k, :Tb], k == 0, k == c.KD - 1, [wh[b], yh], [pQ[b]])
                S.tt("dve", t1[b][:, :Tb], pP[b][:, :Tb], ga[b][:, :Tb], ALU.mult, [pP[b], ga[b]], [t1[b]])
                S.tt("dve", t2[b][:, :Tb], pQ[b][:, :Tb], gb[b][:, :Tb], ALU.mult, [pQ[b], gb[b]], [t2[b]])
                S.tt("dve", mg[:, cc, :Tb], t1[b][:, :Tb], t2[b][:, :Tb], ALU.add, [t1[b], t2[b]], [(mg, cc)])
            mga = [(mg, cc) for cc in range(c.KD)]
            for cc in range(c.KD):
                b = cc % 2
                S.dma(wh[b][:], din["wmo"][l, cc], key=("wh", b), writes=[wh[b]], queue="pool")
                S.dma(ga[b][:, :Tb], self.xT[cc * 128:(cc + 1) * 128, t0:t0 + Tb], key=("ga", b), writes=[ga[b]])
                for k in range(c.KD):
                    S.mm(pZ[b][:, :Tb], wh[b][:, k, :], mg[:, k, :Tb], k == 0, k == c.KD - 1, [wh[b]] + mga, [pZ[b]])
                S.stt("dve", xo[b][:, :Tb], pZ[b][:, :Tb], md[:, 2 * c.KD + cc, r:r + 1], ga[b][:, :Tb], ALU.mult, ALU.add, [pZ[b], md, ga[b]], [xo[b]])
                S.dma(self.xT[cc * 128:(cc + 1) * 128, t0:t0 + Tb], xo[b][:, :Tb], key=("xo", b), reads=[xo[b]])
        S.flush()
        st.close()

    def ffn(self, l):
        c, S, din = self.cfg, self.S, self.din
        st = contextlib.ExitStack()
        T = c.TF
        KF = c.FF // 128
        nbk = self.norm_block(st, T, "f")
        h = nbk["h"]
        a = S.sbuf([128, KF, T], BF16, "a", st)
        wg = [S.sbuf([128, c.KD, 128], F32R, "wg", st) for _ in range(2)]
        wu = [S.sbuf([128, c.KD, 128], F32R, "wu", st) for _ in range(2)]
        wd = [S.sbuf([128, KF, 128], BF16, "wd", st) for _ in range(2)]
        sg = [S.sbuf([128, T], F32, "sg", st) for _ in range(2)]
        xi = [S.sbuf([128, T], F32, "xi", st) for _ in range(2)]
        xo = [S.sbuf([128, T], F32, "xo", st) for _ in range(2)]
        pG = [S.psum([128, T], F32, "pG", st) for _ in range(2)]
        pU = [S.psum([128, T], F32, "pU", st) for _ in range(2)]
        pD = [S.psum([128, T], F32, "pD", st) for _ in range(2)]
        md = self.mods[l]
        for (t0, Tb, r) in c.blocks(T):
            self.norm_run(nbk, t0, Tb, r, self.A2[l], md, 3 * c.KD)
            for j in range(KF):
                b = j % 2
                S.dma(wg[b][:], din["wg"][l, j], key=("wg", b), writes=[wg[b]], queue="pool")
                S.dma(wu[b][:], din["wu"][l, j], key=("wu", b), writes=[wu[b]], queue="pool")
                for k in range(c.KD):
                    S.mm(pG[b][:, :Tb], wg[b][:, k, :], h[:, k, :Tb], k == 0, k == c.KD - 1, [wg[b], h], [pG[b]])
                for k in range(c.KD):
                    S.mm(pU[b][:, :Tb], wu[b][:, k, :], h[:, k, :Tb], k == 0, k == c.KD - 1, [wu[b], h], [pU[b]])
                S.act(sg[b][:, :Tb], pG[b][:, :Tb], AF.Silu, [pG[b]], [sg[b]])
                S.tt("dve", a[:, j, :Tb], sg[b][:, :Tb], pU[b][:, :Tb], ALU.mult, [sg[b], pU[b]], [(a, j)])
            aa = [(a, j) for j in range(KF)]
            for cc in range(c.KD):
                b = cc % 2
                S.dma(wd[b][:], din["wd"][l, cc], key=("wd", b), writes=[wd[b]], queue="pool")
                S.dma(xi[b][:, :Tb], self.xT[cc * 128:(cc + 1) * 128, t0:t0 + Tb], key=("xi", b), writes=[xi[b]])
                for k in range(KF):
                    S.mm(pD[b][:, :Tb], wd[b][:, k, :], a[:, k, :Tb], k == 0, k == KF - 1, [wd[b]] + aa, [pD[b]])
                S.stt("dve", xo[b][:, :Tb], pD[b][:, :Tb], md[:, 5 * c.KD + cc, r:r + 1], xi[b][:, :Tb], ALU.mult, ALU.add, [pD[b], md, xi[b]], [xo[b]])
                S.dma(self.xT[cc * 128:(cc + 1) * 128, t0:t0 + Tb], xo[b][:, :Tb], key=("xo", b), reads=[xo[b]])
        S.flush()
        st.close()

    def final(self):
        c, S, din = self.cfg, self.S, self.din
        st = contextlib.ExitStack()
        T = c.TF
        nbk = self.norm_block(st, T, "z", F32)
        gf = S.sbuf([128, c.KD, 1], F32, "gf", st)
        S.dma(gf[:, :, 0], din["gfin"], key="gf", writes=[gf])
        for (t0, Tb, r) in c.blocks(T):
            if r == 1:
                continue
            self.norm_run(nbk, t0, Tb, 0, gf, None, 0)
            S.dma(self.out[:, t0 - c.LC:t0 - c.LC + Tb].rearrange("(k p) t -> p k t", p=128), nbk["h"][:, :, :Tb],
                  key="outst", reads=[nbk["h"]])
        S.flush()
        st.close()


_CACHE = {}


def run_cfg(cfg, inputs, n_cores=None):
    B = cfg.B
    maps = [host_prepare(cfg, inputs, 0)]
    for b in range(1, B):
        mb = dict(maps[0])
        xT = np.concatenate([inputs["ctx"][b].T, inputs["x"][b].T], axis=1)
        mb["xin"] = np.ascontiguousarray(xT).astype(np.float32)
        cc = np.stack([inputs["c"][b], inputs["c_ctx"]], axis=1)
        mb["cT"] = np.ascontiguousarray(cc.reshape(cfg.KD, 128, 2).transpose(1, 0, 2)).astype(np.float32)
        maps.append(mb)
    shapes = {k: v.shape for k, v in maps[0].items()}
    key = (cfg.D, cfg.L, cfg.LC, cfg.DEPTH)
    if key not in _CACHE:
        import os
        ns = os.environ.get("K_STAGES")
        _CACHE[key] = Prog(cfg, shapes).build(int(ns) if ns else None)
    nc = _CACHE[key]
    res = run_bass_kernel_spmd(nc, maps, core_ids=list(range(B)))
    out = np.stack([np.ascontiguousarray(res.results[b]["out"].T) for b in range(B)])
    return out.astype(np.float32)


def kernel(**inputs):
    inputs = {k: np.asarray(v) for k, v in inputs.items()}
    cfg = Cfg()
    return run_cfg(cfg, inputs)
```
